# Optimizing a Trainium2 kernel written in Bass

```python
import math
import jax, jax.numpy as jnp
from jax import lax
import numpy as np

D_MODEL = 1024
BATCH = 8
SEQ = 2048
DEPTH = 2

HEAD_DIM = 64
D_MIX = D_MODEL
MLSTM_DIM = D_MIX // 4
CONV_DIM = D_MIX // 4
ATTN_DIM = D_MIX // 2
MLSTM_HEADS = MLSTM_DIM // HEAD_DIM
ATTN_HEADS = ATTN_DIM // HEAD_DIM
MLSTM_CHUNK = 64
QK_CONV = 4
CONF_KERNEL = 31
MOBA_BLOCK = 256
MOBA_TOPK = 3
MOBA_Q_CHUNK = 64
REL_BUCKETS = 32
REL_MAX_DIST = 128
D_FF = 2752
FFN_CONV = 3
EPS = 1e-6
NEG = -1e30
P_IN = 4 * MLSTM_DIM + 2 * MLSTM_HEADS + 2 * CONV_DIM + 3 * ATTN_DIM

kernel_name = "hybrid_mlstm_conformer_moba_block"


def _split_points():
    sizes = [MLSTM_DIM] * 4 + [MLSTM_HEADS] * 2 + [CONV_DIM] * 2 + [ATTN_DIM] * 3
    return [int(c) for c in np.cumsum(sizes)[:-1]]


def rmsnorm(x, g):
    xf = x.astype(jnp.float32)
    y = xf * lax.rsqrt(jnp.mean(xf * xf, axis=-1, keepdims=True) + EPS)
    return (y * g.astype(jnp.float32)).astype(x.dtype)


def layernorm(x, g, b):
    xf = x.astype(jnp.float32)
    mu = jnp.mean(xf, axis=-1, keepdims=True)
    var = jnp.mean(jnp.square(xf - mu), axis=-1, keepdims=True)
    y = (xf - mu) * lax.rsqrt(var + EPS)
    return (y * g.astype(jnp.float32) + b.astype(jnp.float32)).astype(x.dtype)


def causal_dwconv(x, w, b):
    width, ch = w.shape
    y = lax.conv_general_dilated(
        x, w[:, None, :], window_strides=(1,), padding=[(width - 1, 0)],
        dimension_numbers=("NWC", "WIO", "NWC"), feature_group_count=ch)
    return y + b


def split_heads(t, n_heads):
    bn, s, _ = t.shape
    return t.reshape(bn, s, n_heads, -1).transpose(0, 2, 1, 3)


def merge_heads(t):
    bn, h, s, d = t.shape
    return t.transpose(0, 2, 1, 3).reshape(bn, s, h * d)


def mlstm_chunkwise(q, k, v, i_pre, f_pre):
    bn, nh, s, d = q.shape
    L = MLSTM_CHUNK
    nc = s // L
    q = q * (d ** -0.5)
    logf = jax.nn.log_sigmoid(f_pre)

    def to_chunks(t):
        return jnp.moveaxis(t.reshape(bn, nh, nc, L, *t.shape[3:]), 2, 0)

    causal = jnp.tril(jnp.ones((L, L), dtype=bool))

    def step(carry, inp):
        C, n, m = carry
        qc, kc, vc, ic, lfc = inp
        b = jnp.cumsum(lfc, axis=-1)
        b_tot = b[..., -1]
        dmat = jnp.where(causal, b[..., :, None] - b[..., None, :] + ic[..., None, :], NEG)
        inter = b + m[..., None]
        m_t = jnp.maximum(inter, jnp.max(dmat, axis=-1))
        w = jnp.exp(dmat - m_t[..., None])
        a_inter = jnp.exp(inter - m_t)
        sc = jnp.einsum("bhtd,bhsd->bhts", qc, kc) * w
        num = (a_inter[..., None] * jnp.einsum("bhvk,bhtk->bhtv", C, qc)
               + jnp.einsum("bhts,bhsv->bhtv", sc, vc))
        den = a_inter * jnp.einsum("bhk,bhtk->bht", n, qc) + jnp.sum(sc, axis=-1)
        h = num / jnp.maximum(jnp.abs(den), jnp.exp(-m_t))[..., None]
        g = b_tot[..., None] - b + ic
        m_new = jnp.maximum(b_tot + m, jnp.max(g, axis=-1))
        wk = jnp.exp(g - m_new[..., None])
        decay = jnp.exp(b_tot + m - m_new)
        C_new = decay[..., None, None] * C + jnp.einsum("bhs,bhsv,bhsk->bhvk", wk, vc, kc)
        n_new = decay[..., None] * n + jnp.einsum("bhs,bhsk->bhk", wk, kc)
        return (C_new, n_new, m_new), h

    init = (jnp.zeros((bn, nh, d, d), jnp.float32),
            jnp.zeros((bn, nh, d), jnp.float32),
            jnp.full((bn, nh), NEG, jnp.float32))
    _, hs = lax.scan(step, init, (to_chunks(q), to_chunks(k), to_chunks(v),
                                  to_chunks(i_pre), to_chunks(logf)))
    return jnp.moveaxis(hs, 0, 2).reshape(bn, nh, s, d)


def t5_bucket(dist):
    n = jnp.maximum(dist, 0)
    max_exact = REL_BUCKETS // 2
    nf = jnp.maximum(n, 1).astype(jnp.float32)
    large = max_exact + (jnp.log(nf / max_exact) / math.log(REL_MAX_DIST / max_exact)
                         * (REL_BUCKETS - max_exact)).astype(jnp.int32)
    large = jnp.minimum(large, REL_BUCKETS - 1)
    return jnp.where(n < max_exact, n, large)


def moba_attention(q, k, v, rel_bias):
    bn, nh, s, d = q.shape
    blk = MOBA_BLOCK
    qc_len = MOBA_Q_CHUNK
    nb = -(-s // blk)
    pad = nb * blk - s
    kp = jnp.pad(k, ((0, 0), (0, 0), (0, pad), (0, 0)))
    vp = jnp.pad(v, ((0, 0), (0, 0), (0, pad), (0, 0)))
    kb = kp.reshape(bn, nh, nb, blk, d)
    vb = vp.reshape(bn, nh, nb, blk, d)
    kmean = jnp.mean(kb, axis=3)
    k_sel = min(MOBA_TOPK, nb)
    scale = d ** -0.5
    bias_t = rel_bias.T
    b_idx = jnp.arange(bn)[:, None, None, None]
    h_idx = jnp.arange(nh)[None, :, None, None]
    h_idx5 = jnp.arange(nh)[None, :, None, None, None]
    nq = s // qc_len

    def one_chunk(c):
        q0 = c * qc_len
        qc = lax.dynamic_slice_in_dim(q, q0, qc_len, axis=2)
        t = q0 + jnp.arange(qc_len)
        own = q0 // blk
        gate = jnp.einsum("bhqd,bhnd->bhqn", qc, kmean).astype(jnp.float32)
        gate = jnp.where(jnp.arange(nb) < own, gate, NEG)
        _, idx = lax.top_k(gate, k_sel)
        valid = jnp.arange(k_sel) < own
        kg = kb[b_idx, h_idx, idx]
        vg = vb[b_idx, h_idx, idx]
        s_pos = idx[..., None] * blk + jnp.arange(blk)
        dist = t[None, None, :, None, None] - s_pos
        s_past = (jnp.einsum("bhqd,bhqjkd->bhqjk", qc, kg).astype(jnp.float32) * scale
                  + bias_t[h_idx5, t5_bucket(dist)].astype(jnp.float32))
        s_past = jnp.where(valid[:, None], s_past, NEG)
        ko = lax.dynamic_slice_in_dim(kp, own * blk, blk, axis=2)
        vo = lax.dynamic_slice_in_dim(vp, own * blk, blk, axis=2)
        dist_o = t[:, None] - (own * blk + jnp.arange(blk))[None, :]
        s_own = (jnp.einsum("bhqd,bhkd->bhqk", qc, ko).astype(jnp.float32) * scale
                 + bias_t[:, t5_bucket(dist_o)][None].astype(jnp.float32))
        s_own = jnp.where(dist_o >= 0, s_own, NEG)
        logits = jnp.concatenate([s_past.reshape(bn, nh, qc_len, k_sel * blk), s_own], axis=-1)
        p = jax.nn.softmax(logits, axis=-1).astype(v.dtype)
        p_past = p[..., :k_sel * blk].reshape(bn, nh, qc_len, k_sel, blk)
        p_own = p[..., k_sel * blk:]
        return (jnp.einsum("bhqjk,bhqjkd->bhqd", p_past, vg)
                + jnp.einsum("bhqk,bhkd->bhqd", p_own, vo))

    outs = lax.map(one_chunk, jnp.arange(nq))
    return jnp.moveaxis(outs, 0, 2).reshape(bn, nh, s, d)


def hybrid_layer(x, norm1_g, w_in, mlstm_qk_conv_w, mlstm_qk_conv_b, mlstm_ig_b,
                 mlstm_fg_b, mlstm_head_g, conf_dw_w, conf_dw_b, conf_ln_g, conf_ln_b,
                 rel_bias, w_out, norm2_g, ffn_w_up, ffn_conv_w, ffn_conv_b, ffn_w_down):
    f32 = jnp.float32
    h = rmsnorm(x, norm1_g)
    z = h @ w_in
    (m_q, m_k, m_v, m_o, m_i, m_f, c_a, c_g, a_q, a_k, a_v) = jnp.split(z, _split_points(), axis=-1)

    qk = jax.nn.silu(causal_dwconv(jnp.concatenate([m_q, m_k], axis=-1),
                                   mlstm_qk_conv_w, mlstm_qk_conv_b))
    m_q, m_k = jnp.split(qk, 2, axis=-1)
    hm = mlstm_chunkwise(split_heads(m_q, MLSTM_HEADS).astype(f32),
                         split_heads(m_k, MLSTM_HEADS).astype(f32),
                         split_heads(m_v, MLSTM_HEADS).astype(f32),
                         (m_i + mlstm_ig_b).astype(f32).transpose(0, 2, 1),
                         (m_f + mlstm_fg_b).astype(f32).transpose(0, 2, 1))
    mu = jnp.mean(hm, axis=-1, keepdims=True)
    var = jnp.mean(jnp.square(hm - mu), axis=-1, keepdims=True)
    hm = (hm - mu) * lax.rsqrt(var + EPS)
    y_m = (merge_heads(hm) * mlstm_head_g.astype(f32)).astype(x.dtype) * jax.nn.sigmoid(m_o)

    u = c_a * jax.nn.sigmoid(c_g)
    u = causal_dwconv(u, conf_dw_w, conf_dw_b)
    y_c = jax.nn.silu(layernorm(u, conf_ln_g, conf_ln_b))

    y_a = merge_heads(moba_attention(split_heads(a_q, ATTN_HEADS),
                                     split_heads(a_k, ATTN_HEADS),
                                     split_heads(a_v, ATTN_HEADS), rel_bias))

    x = x + jnp.concatenate([y_m, y_c, y_a], axis=-1) @ w_out

    u = causal_dwconv(rmsnorm(x, norm2_g) @ ffn_w_up, ffn_conv_w, ffn_conv_b)
    g, val = jnp.split(u, 2, axis=-1)
    return x + (jax.nn.silu(g) * val) @ ffn_w_down


def setup_inputs(seed: int = 0) -> dict:
    key = jax.random.key(seed)
    ks = jax.random.split(key, 24)

    def nrm(k, shape, s):
        return jax.random.normal(k, shape, jnp.float32) * s

    fg_base = jnp.linspace(3.0, 6.0, MLSTM_HEADS, dtype=jnp.float32)
    return {
        "x": nrm(ks[0], (BATCH, SEQ, D_MODEL), 1.0),
        "norm1_g": 1.0 + nrm(ks[1], (DEPTH, D_MODEL), 0.05),
        "w_in": nrm(ks[2], (DEPTH, D_MODEL, P_IN), D_MODEL ** -0.5),
        "mlstm_qk_conv_w": nrm(ks[3], (DEPTH, QK_CONV, 2 * MLSTM_DIM), QK_CONV ** -0.5),
        "mlstm_qk_conv_b": nrm(ks[4], (DEPTH, 2 * MLSTM_DIM), 0.02),
        "mlstm_ig_b": nrm(ks[5], (DEPTH, MLSTM_HEADS), 0.1),
        "mlstm_fg_b": fg_base[None, :] + nrm(ks[6], (DEPTH, MLSTM_HEADS), 0.1),
        "mlstm_head_g": 1.0 + nrm(ks[7], (DEPTH, MLSTM_DIM), 0.05),
        "conf_dw_w": nrm(ks[8], (DEPTH, CONF_KERNEL, CONV_DIM), CONF_KERNEL ** -0.5),
        "conf_dw_b": nrm(ks[9], (DEPTH, CONV_DIM), 0.02),
        "conf_ln_g": 1.0 + nrm(ks[10], (DEPTH, CONV_DIM), 0.05),
        "conf_ln_b": nrm(ks[11], (DEPTH, CONV_DIM), 0.02),
        "rel_bias": nrm(ks[12], (REL_BUCKETS, ATTN_HEADS), 0.2),
        "w_out": nrm(ks[13], (DEPTH, D_MIX, D_MODEL), D_MIX ** -0.5),
        "norm2_g": 1.0 + nrm(ks[14], (DEPTH, D_MODEL), 0.05),
        "ffn_w_up": nrm(ks[15], (DEPTH, D_MODEL, 2 * D_FF), D_MODEL ** -0.5),
        "ffn_conv_w": nrm(ks[16], (DEPTH, FFN_CONV, 2 * D_FF), FFN_CONV ** -0.5),
        "ffn_conv_b": nrm(ks[17], (DEPTH, 2 * D_FF), 0.02),
        "ffn_w_down": nrm(ks[18], (DEPTH, D_FF, D_MODEL), D_FF ** -0.5),
        "final_g": 1.0 + nrm(ks[19], (D_MODEL,), 0.05),
    }


def reference(x, norm1_g, w_in, mlstm_qk_conv_w, mlstm_qk_conv_b, mlstm_ig_b, mlstm_fg_b,
              mlstm_head_g, conf_dw_w, conf_dw_b, conf_ln_g, conf_ln_b, rel_bias, w_out,
              norm2_g, ffn_w_up, ffn_conv_w, ffn_conv_b, ffn_w_down, final_g):
    for l in range(DEPTH):
        x = hybrid_layer(x, norm1_g[l], w_in[l], mlstm_qk_conv_w[l], mlstm_qk_conv_b[l],
                         mlstm_ig_b[l], mlstm_fg_b[l], mlstm_head_g[l], conf_dw_w[l],
                         conf_dw_b[l], conf_ln_g[l], conf_ln_b[l], rel_bias, w_out[l],
                         norm2_g[l], ffn_w_up[l], ffn_conv_w[l], ffn_conv_b[l], ffn_w_down[l])
    return rmsnorm(x, final_g)
```

```python
import math
import os
from contextlib import ExitStack
import numpy as np
import ml_dtypes
import concourse.bass as bass
import concourse.mybir as mybir
from concourse.bass_utils import run_bass_kernel_spmd

F32 = mybir.dt.float32
BF16 = mybir.dt.bfloat16
AF = mybir.ActivationFunctionType
ALU = mybir.AluOpType
AX = mybir.AxisListType

S_ = 2048
D_ = 1024
NT = 16
P_IN = 3080
DFF = 2752
NEG = -1.0e30
EPS = 1e-6
NJ = 22


class Sched:
    ENGS = ("pe", "act", "dve", "pool", "sp")

    def __init__(self, nc):
        self.nc = nc
        self.instrs = []
        self.streams = {e: [] for e in self.ENGS}
        self.last_w = {}
        self.readers = {}
        self.seen = {e: {f: -1 for f in self.ENGS} for e in self.ENGS}
        self.seen_dma = {e: {} for e in self.ENGS}
        self.chan_count = {}
        self.chans = []
        self.no_barrier = set()
        self.mute = False
        self.psx_last = {}

    def op(self, eng, fn, reads=(), writes=(), chan=None, extra_deps=(), exact=False):
        if self.mute:
            return -1
        idx = len(self.instrs)
        deps = set(extra_deps)
        for r in reads:
            if isinstance(r, tuple) and len(r) == 2 and r[0] == "ps":
                last = self.psx_last.get(r[1])
                if last is not None and self.instrs[last]["eng"] != eng:
                    deps.add(last)
                self.psx_last[r[1]] = idx
        for r in reads:
            w = self.last_w.get(r)
            if w is not None:
                deps.add(w)
        for w_ in writes:
            w = self.last_w.get(w_)
            if w is not None:
                deps.add(w)
            for r in self.readers.get(w_, ()):
                deps.add(r)
        rec = dict(idx=idx, eng=eng, fn=fn, waits=[], signal=False, chan=chan,
                   pos=len(self.streams[eng]), exact=exact)
        if chan is not None and chan not in self.chan_count:
            self.chan_count[chan] = 0
            self.chans.append(chan)
        for d in sorted(deps):
            J = self.instrs[d]
            if J["chan"] is not None:
                c = J["chan"]
                val = J["dma_val"] if J["exact"] else self.chan_count[c]
                if self.seen_dma[eng].get(c, 0) >= val:
                    continue
                self.seen_dma[eng][c] = val
                rec["waits"].append(("dma", c, val))
            else:
                F = J["eng"]
                if F == "pe" and eng == "pe":
                    continue
                if self.seen[eng][F] >= J["pos"]:
                    continue
                self.seen[eng][F] = J["pos"]
                J["signal"] = True
                rec["waits"].append(("eng", F, d))
        if chan is not None:
            self.chan_count[chan] += 16
            rec["dma_val"] = self.chan_count[chan]
        self.instrs.append(rec)
        self.streams[eng].append(idx)
        for r in reads:
            self.readers.setdefault(r, []).append(idx)
        for w_ in writes:
            self.last_w[w_] = idx
            self.readers[w_] = []
        return idx

    def barrier(self):
        if self.mute:
            return
        lasts = []
        for e in self.ENGS:
            for idx in reversed(self.streams[e]):
                if self.instrs[idx]["fn"] is not None:
                    lasts.append(idx)
                    break
        chan_state = {c: v for c, v in self.chan_count.items() if c not in self.no_barrier}
        for e in self.ENGS:
            if e == "sp" and not self.streams[e]:
                continue
            idx = self.op(e, None, extra_deps=[l for l in lasts if self.instrs[l]["chan"] is None])
            rec = self.instrs[idx]
            for c, v in chan_state.items():
                if v > 0 and self.seen_dma[e].get(c, 0) < v:
                    self.seen_dma[e][c] = v
                    rec["waits"].append(("dma", c, v))

    def emit(self, sems, chan_sems, block):
        for e in self.ENGS:
            c = 0
            for idx in self.streams[e]:
                r = self.instrs[idx]
                if r["chan"] is None and r["signal"]:
                    c += 1
                r["count"] = c
        instrs = self.instrs

        def run_stream(e):
            def body(engine):
                waited = {}
                pending_sig = 0
                for idx in self.streams[e]:
                    r = instrs[idx]
                    need = {}
                    for w in r["waits"]:
                        if w[0] == "dma":
                            key = ("c", w[1]); val = w[2]
                        else:
                            key = ("e", w[1]); val = instrs[w[2]]["count"]
                        need[key] = max(need.get(key, 0), val)
                    for key, val in need.items():
                        if waited.get(key, 0) >= val:
                            continue
                        waited[key] = val
                        s = chan_sems[key[1]] if key[0] == "c" else sems[key[1]]
                        engine.wait_ge(s, val)
                    if r["fn"] is None:
                        assert not r["signal"]
                        continue
                    ins = r["fn"](engine)
                    if r["chan"] is not None:
                        ins.then_inc(chan_sems[r["chan"]], 16)
                    elif r["signal"]:
                        ins.then_inc(sems[e], 1)
            return body

        deco = {"pe": block.tensor, "act": block.scalar, "dve": block.vector,
                "pool": block.gpsimd, "sp": block.sync}
        for e in self.ENGS:
            if self.streams[e]:
                deco[e](run_stream(e))


def _t5_bucket_np(dist):
    n = np.maximum(dist, 0)
    max_exact = 16
    nf = np.maximum(n, 1).astype(np.float32)
    large = max_exact + (np.log(nf / max_exact) / math.log(128 / max_exact) * (32 - max_exact)).astype(np.int32)
    large = np.minimum(large, 31)
    return np.where(n < max_exact, n, large)


GU = 384


def host_constants():
    c = {}
    c["ident"] = np.eye(128, dtype=np.float32)
    s = np.arange(128)[:, None]
    t = np.arange(128)[None, :]
    c["tri125"] = np.where(s <= t, 0.125, 0.0).astype(np.float32)
    oh = np.zeros((8, S_), dtype=np.float32)
    for n in range(8):
        oh[n, n * 256:(n + 1) * 256] = 1.0
    c["blk_onehot"] = oh.astype(ml_dtypes.bfloat16)
    sel = np.zeros((4, 128), dtype=np.float32)
    for h in range(4):
        sel[h, (h % 2) * 64:(h % 2) * 64 + 64] = 1.0
    c["sel4"] = sel
    dl = np.zeros((4, 16, 2), dtype=np.float32)
    for h in range(4):
        dl[h, :, h // 2] = 1.0
    c["pairmask"] = dl
    return c


def t5_tables(rel_bias):
    p = np.arange(128)[:, None]
    u = np.arange(GU)[None, :]
    dist = u - p
    b = _t5_bucket_np(dist)
    G = rel_bias[b]
    G = np.where((dist >= 0)[:, :, None], G, np.float32(NEG))
    G = np.ascontiguousarray(np.transpose(G, (0, 2, 1))).astype(np.float32)
    b31 = np.ascontiguousarray(np.broadcast_to(rel_bias[31][None, :], (128, 8))).astype(np.float32)
    return G, b31


def build(n_layers=2, debug=None, phases=None):
    nc = bass.Bass("TRN2", target_bir_lowering=False)
    L = n_layers

    def din(name, shape, dt=F32):
        return nc.dram_tensor(name, list(shape), dt, kind="ExternalInput").ap()

    x_d = din("x", [S_, D_])
    w_in_d = din("w_in", [2, D_, P_IN])
    w_out_d = din("w_out", [2, D_, D_])
    w_up_d = din("ffn_w_up", [2, D_, 2 * DFF])
    w_dn_d = din("ffn_w_down", [2, DFF, D_])
    norm1_d = din("norm1_g", [2, D_])
    norm2_d = din("norm2_g", [2, D_])
    final_d = din("final_g", [1, D_])
    qkw_d = din("qkw", [2, 128, 4, 4])
    qkb_d = din("qkb", [2, 128, 4])
    igb_d = din("igb", [2, 4, 1])
    fgb_d = din("fgb", [2, 4, 1])
    hg_d = din("headg", [2, 256])
    cw_d = din("cfw", [2, 128, 2, 31])
    cvec_d = din("cvec", [2, 128, 3, 2])
    fcw_d = din("fcw", [2, 128, 2 * NJ, 3])
    fcb_d = din("fcb", [2, 128, 2 * NJ])
    G_d = din("gt5", [128, 8, GU])
    b31_d = din("b31", [128, 8])
    ident_d = din("ident", [128, 128])
    tri_d = din("tri125", [128, 128])
    oneh_d = din("blk_onehot", [8, S_], BF16)
    sel4_d = din("sel4", [4, 128])
    pm_d = din("pairmask", [4, 16, 2])
    out_d = nc.dram_tensor("out", [S_, D_], F32, kind="ExternalOutput").ap()
    dbg_d = None
    if debug == "mix":
        dbg_d = nc.dram_tensor("dbg", [128, 8, S_], BF16, kind="ExternalOutput").ap()
    if debug in ("x1", "x2"):
        dbg_d = nc.dram_tensor("dbg", [S_, D_], F32, kind="ExternalOutput").ap()

    wb_in = [nc.dram_tensor(f"wb_in{l}", [D_, P_IN], BF16, kind="Internal").ap() for l in range(L)]
    wb_out = [nc.dram_tensor(f"wb_out{l}", [D_, D_], BF16, kind="Internal").ap() for l in range(L)]
    wb_up = [nc.dram_tensor(f"wb_up{l}", [D_, 2 * DFF], BF16, kind="Internal").ap() for l in range(L)]
    wb_dn = [nc.dram_tensor(f"wb_dn{l}", [DFF, D_], BF16, kind="Internal").ap() for l in range(L)]

    S = Sched(nc)
    es = ExitStack()
    E = es.enter_context
    sb = lambda name, shape, dt: E(nc.sbuf_tensor("sb_" + name, list(shape), dt))

    x_sb = sb("x_sb", [128, NT, D_], F32)
    xT = sb("xT", [128, 8, S_], BF16)
    ident_f = sb("ident_f", [128, 128], F32)
    ident_b = sb("ident_b", [128, 128], BF16)
    ones_f = sb("ones_f", [128, 128], F32)
    tri125 = sb("tri125", [128, 128], F32)
    G_sb = sb("G_sb", [128, 8, GU], BF16)
    b31_sb = sb("b31_sb", [128, 8], F32)
    g_rep = sb("g_rep", [128, D_], F32)
    ss = sb("ss", [128, NT], F32)
    rstd = sb("rstd", [128, NT], F32)
    onecol = sb("onecol", [128, 2], BF16)
    sel4 = sb("sel4", [4, 128], F32)
    pmask = sb("pmask", [4, 16, 2], F32)
    PS = [E(nc.psum_tensor(f"ps{i}", [128, 512], F32)) for i in range(8)]
    PSB = [p.bitcast(BF16) for p in PS]
    sems = {e: E(nc.semaphore(f"s_{e}")) for e in Sched.ENGS}
    NCHAN = 40
    chan_sem_list = [E(nc.semaphore(f"c{i}")) for i in range(NCHAN)]
    block = E(nc.Block())

    pk = lambda i: ("ps", i)

    def dma(q, out, in_, R, W, chan, **kw):
        return S.op(q, lambda e: e.dma_start(out=out, in_=in_, **kw), reads=R, writes=W, chan=chan)

    def mm(out, lhsT, rhs, start, stop, R, W):
        return S.op("pe", lambda e: e.matmul(out, lhsT=lhsT, rhs=rhs, start=start, stop=stop), reads=R, writes=W)

    def tr(out, in_, ident, R, W):
        return S.op("pe", lambda e: e.transpose(out, in_, ident), reads=R, writes=W)

    def act(out, in_, func, R, W, bias=0.0, scale=1.0, accum_out=None):
        if accum_out is None:
            return S.op("act", lambda e: e.activation(out=out, in_=in_, func=func, bias=bias, scale=scale), reads=R, writes=W)
        return S.op("act", lambda e: e.activation(out=out, in_=in_, func=func, bias=bias, scale=scale, accum_out=accum_out), reads=R, writes=W)

    def tt(eng, out, in0, in1, op, R, W):
        return S.op(eng, lambda e: e.tensor_tensor(out=out, in0=in0, in1=in1, op=op), reads=R, writes=W)

    def ts(eng, out, in0, s1, s2, op0, op1, R, W):
        if s2 is None:
            return S.op(eng, lambda e: e.tensor_scalar(out=out, in0=in0, scalar1=s1, scalar2=None, op0=op0), reads=R, writes=W)
        return S.op(eng, lambda e: e.tensor_scalar(out=out, in0=in0, scalar1=s1, scalar2=s2, op0=op0, op1=op1), reads=R, writes=W)

    def stt(out, in0, scalar, in1, op0, op1, R, W):
        return S.op("dve", lambda e: e.scalar_tensor_tensor(out=out, in0=in0, scalar=scalar, in1=in1, op0=op0, op1=op1), reads=R, writes=W)

    def cp(eng, out, in_, R, W):
        if eng == "act":
            return act(out, in_, AF.Copy, R, W)
        return S.op(eng, lambda e: e.tensor_copy(out=out, in_=in_), reads=R, writes=W)

    def memset(eng, ap, val, W):
        return S.op(eng, lambda e: e.memset(ap, val), reads=(), writes=W)

    def recip(out, in_, R, W):
        return S.op("dve", lambda e: e.reciprocal(out=out, in_=in_), reads=R, writes=W)

    def scan(out, d0, d1, init, op0, op1, R, W):
        return S.op("dve", lambda e: e.tensor_tensor_scan(out=out, data0=d0, data1=d1, initial=init, op0=op0, op1=op1), reads=R, writes=W)

    def treduce(out, in_, op, R, W):
        return S.op("dve", lambda e: e.tensor_reduce(out=out, in_=in_, axis=AX.X, op=op), reads=R, writes=W)

    def tsingle(out, in_, scalar, op, R, W):
        return S.op("dve", lambda e: e.tensor_single_scalar(out=out, in_=in_, scalar=scalar, op=op), reads=R, writes=W)

    def vmax(out, in_, R, W):
        return S.op("dve", lambda e: e.max(out=out, in_=in_), reads=R, writes=W)

    chan_ids = {}

    def chan(name):
        if name not in chan_ids:
            assert len(chan_ids) < NCHAN, "out of dma channels"
            chan_ids[name] = f"c{len(chan_ids)}"
        return chan_ids[name]

    def bc(ap, shape):
        return ap.to_broadcast(list(shape))

    cast_state = {"n": 0, "last": {}}

    def cast_w(src, dst, rows, key, c0=None, c1=None):
        if c0 is not None:
            src = src[:, c0:c1]
            dst = dst[:, c0:c1]
        for r0 in range(0, rows, 128):
            r1 = min(rows, r0 + 128)
            ci = cast_state["n"] % 4
            cast_state["n"] += 1
            ch = chan(f"cast{ci}")
            S.no_barrier.add(ch)
            prev = cast_state["last"].get(ci)
            xd = [prev] if prev is not None else list(cast_state.get("xdma", []))
            idx = S.op("pool", lambda e, o=dst[r0:r1, :], i_=src[r0:r1, :]: e.dma_start(out=o, in_=i_), reads=[], writes=[(key, r0 // 128)],
                       chan=ch, extra_deps=xd, exact=True)
            cast_state["last"][ci] = idx

    x_dma = []
    for i4 in range(4):
        x_dma.append(dma("sp", x_sb[:, i4 * 4:(i4 + 1) * 4, :],
                         x_d[i4 * 512:(i4 + 1) * 512, :].rearrange("(i p) d -> p i d", p=128), [], [("x", i4 * 4 + k) for k in range(4)], chan("xin")))
    x_dma.append(dma("sp", g_rep[:], norm1_d[0, :].partition_broadcast(128), [], ["g_rep"], chan("grep")))
    cast_state["xdma"] = x_dma
    cast_state["g_preloaded"] = True
    dma("sp", ident_f[:], ident_d, [], ["ident_f"], chan("const"))
    dma("sp", tri125[:], tri_d, [], ["tri125"], chan("const"))
    dma("sp", b31_sb[:], b31_d, [], ["b31"], chan("const"))
    dma("sp", sel4[:], sel4_d, [], ["sel4"], chan("const"))
    dma("sp", pmask[:], pm_d, [], ["pmask"], chan("const"))
    for l in range(L):
        cast_w(w_in_d[l], wb_in[l], D_, f"wbinC{l}", 1032, 1544)
        cast_w(w_in_d[l], wb_in[l], D_, f"wbinM{l}", 0, 1032)
        cast_w(w_in_d[l], wb_in[l], D_, f"wbinA{l}", 1544, 3080)
        cast_w(w_out_d[l], wb_out[l], D_, f"wbout{l}")
        cast_w(w_up_d[l], wb_up[l], D_, f"wbup{l}")
        cast_w(w_dn_d[l], wb_dn[l], DFF, f"wbdn{l}")
    WBIN = lambda l, grp: [(f"wbin{grp}{l}", i) for i in range(8)]
    WBOUT = lambda l: [(f"wbout{l}", i) for i in range(8)]
    WBUP = lambda l: [(f"wbup{l}", i) for i in range(8)]
    WBDN = lambda l: [(f"wbdn{l}", i) for i in range(22)]

    cp("dve", ident_b[:], ident_f[:], ["ident_f"], ["ident_b"])
    memset("dve", ones_f[:], 1.0, ["ones_f"])
    memset("dve", onecol[:], 1.0 / 256.0, ["onecol"])
    with nc.sbuf_tensor("G_tmp", [128, 8, GU], F32) as G_tmp:
        dma("sp", G_tmp[:], G_d, [], ["G_tmp"], chan("const"))
        tt("dve", G_sb[:], G_tmp[:], bc(b31_sb[:].unsqueeze(2), [128, 8, GU]), ALU.subtract, ["G_tmp", "b31"], ["G_sb"])
        S.barrier()

    def norm_to_xT(g_dram_row, tag, pbase=0, use_pool=True):
        if cast_state.pop("g_preloaded", False):
            pass
        else:
            dma("sp", g_rep[:], g_dram_row.partition_broadcast(128), [], ["g_rep"], chan("grep"))
        with nc.sbuf_tensor(f"sqj_{tag}", [128, 2, D_], BF16) as sqj, \
             nc.sbuf_tensor(f"xs_{tag}", [128, 2, D_], BF16) as xs, \
             nc.sbuf_tensor(f"xf_{tag}", [128, 2, D_], F32) as xf:
            def s1(i):
                b = i % 2
                act(sqj[:, b, :], x_sb[:, i, :], AF.Square, [("x", i)], [("sqj", b), ("ss", i)], accum_out=ss[:, i:i + 1])
                act(rstd[:, i:i + 1], ss[:, i:i + 1], AF.Sqrt, [("ss", i)], [("rstd", i)], scale=1.0 / D_, bias=EPS)
                recip(rstd[:, i:i + 1], rstd[:, i:i + 1], [("rstd", i)], [("rstd", i)])

            def s2(i):
                b = i % 2
                if use_pool:
                    ts("pool", xf[:, b, :], x_sb[:, i, :], rstd[:, i:i + 1], 1.0, ALU.mult, ALU.mult, [("x", i), ("rstd", i)], [("xf", b)])
                    tt("pool", xs[:, b, :], xf[:, b, :], g_rep[:], ALU.mult, [("xf", b), "g_rep"], [("xs", b)])
                else:
                    stt(xs[:, b, :], x_sb[:, i, :], rstd[:, i:i + 1], g_rep[:], ALU.mult, ALU.mult,
                        [("x", i), ("rstd", i), "g_rep"], [("xs", b)])
                pb = pbase + i % 2
                for k in range(8):
                    tr(PSB[pb][:, k * 128:(k + 1) * 128], xs[:, b, k * 128:(k + 1) * 128], ident_b[:],
                       [("xs", b), "ident_b"], [pk(pb)])
                cp("act", xT[:, :, i * 128:(i + 1) * 128], PSB[pb][:].rearrange("p (k t) -> p k t", k=8),
                   [pk(pb)], [("xT", i)])

            s1(0)
            s1(1)
            s1(2)
            for i in range(NT):
                if i + 3 < NT:
                    s1(i + 3)
                s2(i)
            S.barrier()

    XT_ALL = [("xT", i) for i in range(NT)]

    for l in range(L):
        norm_to_xT(norm1_d[l, :], f"n1_{l}", use_pool=(l > 0))
        with ExitStack() as les:
            LE = les.enter_context
            mixT = LE(nc.sbuf_tensor(f"mixT{l}", [128, 8, S_], BF16))

            S.mute = phases is not None and 'conv' not in phases
            with ExitStack() as ces:
                CE = ces.enter_context
                wC = CE(nc.sbuf_tensor(f"wC{l}", [128, 8, 512], BF16))
                if l == 0:
                    with nc.sbuf_tensor("wC32", [128, 8, 512], F32) as wC32:
                        dma("sp", wC32[:], w_in_d[0][:, 1032:1544].rearrange("(k p) n -> p k n", p=128), [], ["wC32"], chan("wC"))
                        cp("dve", wC[:, 0:4, :], wC32[:, 0:4, :], ["wC32"], ["wC"])
                        cp("act", wC[:, 4:8, :], wC32[:, 4:8, :], ["wC32"], ["wC"])
                        S.barrier()
                ucv = CE(nc.sbuf_tensor(f"ucv{l}", [128, 2, 30 + S_], BF16))
                dgc = CE(nc.sbuf_tensor(f"dgc{l}", [128, 2, 31, 128], BF16))
                cwp = CE(nc.sbuf_tensor(f"cwp{l}", [128, 2, 31], F32))
                cvec = CE(nc.sbuf_tensor(f"cvec{l}", [128, 3, 2], F32))
                sg = CE(nc.sbuf_tensor(f"sg{l}", [128, 2, 512], F32))
                cv = CE(nc.sbuf_tensor(f"cv{l}", [128, 2, 512], F32))
                cvsq = CE(nc.sbuf_tensor(f"cvsq{l}", [128, 2, 512], F32))
                mean = CE(nc.sbuf_tensor(f"mean{l}", [128, 512], F32))
                var = CE(nc.sbuf_tensor(f"var{l}", [128, 512], F32))
                t1 = CE(nc.sbuf_tensor(f"t1{l}", [128, 2, 512], F32))
                if l > 0:
                    dma("sp", wC[:], wb_in[l][:, 1032:1544].rearrange("(k p) n -> p k n", p=128), WBIN(l, "C"), ["wC"], chan("wC"))
                dma("sp", cwp[:], cw_d[l], [], ["cwp"], chan("small"))
                dma("sp", cvec[:], cvec_d[l], [], ["cvec"], chan("small"))
                memset("dve", ucv[:, :, 0:30], 0.0, [("ucv", -1)])
                for fc in range(2):
                    for j in range(31):
                        ts("dve", dgc[:, fc, j, :], ident_f[:], cwp[:, fc, j:j + 1], None, ALU.mult, None,
                           ["ident_f", "cwp"], [("dgc", fc)])
                for c in range(4):
                    for fc in range(2):
                        pa, pg = (0, 1) if (c * 2 + fc) % 2 == 0 else (2, 3)
                        for k in range(8):
                            mm(PS[pa][:], wC[:, k, fc * 128:(fc + 1) * 128], xT[:, k, c * 512:(c + 1) * 512], k == 0, k == 7,
                               ["wC"] + XT_ALL[c * 4:c * 4 + 4], [pk(pa)])
                        for k in range(8):
                            mm(PS[pg][:], wC[:, k, 256 + fc * 128:256 + (fc + 1) * 128], xT[:, k, c * 512:(c + 1) * 512], k == 0, k == 7,
                               ["wC"] + XT_ALL[c * 4:c * 4 + 4], [pk(pg)])
                        sgb = (c * 2 + fc) % 2
                        act(sg[:, sgb, :], PS[pg][:], AF.Sigmoid, [pk(pg)], [("sg", sgb)])
                        tt("dve", ucv[:, fc, 30 + c * 512:30 + (c + 1) * 512], PS[pa][:], sg[:, sgb, :], ALU.mult,
                           [pk(pa), ("sg", sgb)], [("ucv", c)])
                for c in range(4):
                    rd = [("ucv", cc) for cc in range(-1, c + 1)]
                    for fc in range(2):
                        pc = 4 + fc
                        for j in range(31):
                            mm(PS[pc][:], dgc[:, fc, j, :], ucv[:, fc, c * 512 + j:c * 512 + j + 512], j == 0, j == 30,
                               rd + [("dgc", fc)], [pk(pc)])
                        act(cv[:, fc, :], PS[pc][:], AF.Identity, [pk(pc), "cvec"], [("cv", fc)], bias=cvec[:, 0, fc:fc + 1])
                        act(cvsq[:, fc, :], cv[:, fc, :], AF.Square, [("cv", fc)], [("cvsq", fc)])
                    for fc in range(2):
                        mm(PS[6][:], ones_f[:], cv[:, fc, :], fc == 0, fc == 1, ["ones_f", ("cv", fc)], [pk(6)])
                    for fc in range(2):
                        mm(PS[7][:], ones_f[:], cvsq[:, fc, :], fc == 0, fc == 1, ["ones_f", ("cvsq", fc)], [pk(7)])
                    ts("dve", mean[:], PS[6][:], 1.0 / 256, None, ALU.mult, None, [pk(6)], ["mean"])
                    ts("dve", var[:], PS[7][:], 1.0 / 256, EPS, ALU.mult, ALU.add, [pk(7)], ["var"])
                    tt("dve", t1[:, 0, :], mean[:], mean[:], ALU.mult, ["mean"], [("t1", 0)])
                    tt("dve", var[:], var[:], t1[:, 0, :], ALU.subtract, ["var", ("t1", 0)], ["var"])
                    act(var[:], var[:], AF.Sqrt, ["var"], ["var"])
                    recip(var[:], var[:], ["var"], ["var"])
                    for fc in range(2):
                        tt("dve", t1[:, fc, :], cv[:, fc, :], mean[:], ALU.subtract, [("cv", fc), "mean"], [("t1", fc)])
                        tt("dve", t1[:, fc, :], t1[:, fc, :], var[:], ALU.mult, [("t1", fc), "var"], [("t1", fc)])
                        act(mixT[:, 2 + fc, c * 512:(c + 1) * 512], t1[:, fc, :], AF.Silu, [("t1", fc), "cvec"], [("mixT", 2 + fc)],
                            bias=cvec[:, 2, fc:fc + 1], scale=cvec[:, 1, fc:fc + 1])
                S.barrier()

            S.mute = phases is not None and 'mlstm' not in phases
            with ExitStack() as mes:
                ME = mes.enter_context
                tok_sc = ME(nc.sbuf_tensor(f"tok_sc{l}", [128, NT, 8], F32))
                gam128 = ME(nc.sbuf_tensor(f"gam128{l}", [128, 16, 2], F32))
                hg_rep_t = ME(nc.sbuf_tensor(f"hg_rep{l}", [128, 512], F32))
                hg_rep = hg_rep_t[:, 0:256]
                wIF = ME(nc.sbuf_tensor(f"wIF{l}", [128, 8, 8], BF16))
                dma("sp", hg_rep, hg_d[l, :].partition_broadcast(128), [], ["hg_rep"], chan("small"))
                dma("sp", wIF[:], wb_in[l][:, 1024:1032].rearrange("(k p) n -> p k n", p=128), WBIN(l, "M"), ["wIF"], chan("wIF"))

                with ExitStack() as ges:
                    GE = ges.enter_context
                    gi = GE(nc.sbuf_tensor(f"gi{l}", [4, S_], F32))
                    gf = GE(nc.sbuf_tensor(f"gf{l}", [4, S_], F32))
                    gB = GE(nc.sbuf_tensor(f"gB{l}", [4, S_], F32))
                    gM = GE(nc.sbuf_tensor(f"gM{l}", [4, S_], F32))
                    gb2 = GE(nc.sbuf_tensor(f"gb2{l}", [4, 2], F32))
                    one4 = GE(nc.sbuf_tensor(f"one4{l}", [4, 1], F32))
                    gam = GE(nc.sbuf_tensor(f"gam{l}", [4, 16], F32))
                    gR = GE(nc.sbuf_tensor(f"gR{l}", [4, 16, 2], F32))
                    dma("sp", gb2[:, 0:1], igb_d[l], [], ["gb2"], chan("small"))
                    dma("sp", gb2[:, 1:2], fgb_d[l], [], ["gb2"], chan("small"))
                    memset("dve", one4[:], 1.0, ["one4"])
                    for c in range(4):
                        for k in range(8):
                            mm(PS[0][0:4, :], wIF[:, k, 0:4], xT[:, k, c * 512:(c + 1) * 512], k == 0, k == 7,
                               ["wIF"] + XT_ALL[c * 4:c * 4 + 4], [pk(0)])
                        for k in range(8):
                            mm(PS[1][0:4, :], wIF[:, k, 4:8], xT[:, k, c * 512:(c + 1) * 512], k == 0, k == 7,
                               ["wIF"] + XT_ALL[c * 4:c * 4 + 4], [pk(1)])
                        act(gi[:, c * 512:(c + 1) * 512], PS[0][0:4, :], AF.Identity, [pk(0), "gb2"], ["gi"], bias=gb2[:, 0:1])
                        act(gf[:, c * 512:(c + 1) * 512], PS[1][0:4, :], AF.Identity, [pk(1), "gb2"], ["gf"], bias=gb2[:, 1:2])
                    act(gf[:], gf[:], AF.Exp, ["gf"], ["gf"], scale=-1.0)
                    act(gf[:], gf[:], AF.Ln, ["gf"], ["gf"], bias=1.0)
                    scan(gB[:], bc(one4[:], [4, S_]), gf[:], 0.0, ALU.mult, ALU.subtract, ["gf", "one4"], ["gB"])
                    tt("dve", gi[:], gi[:], gB[:], ALU.subtract, ["gi", "gB"], ["gi"])
                    scan(gM[:], bc(one4[:], [4, S_]), gi[:], NEG, ALU.mult, ALU.max, ["gi", "one4"], ["gM"])
                    Mc = gM[:].rearrange("p (c t) -> p c t", t=128)[:, :, 127:128]
                    gi3 = gi[:].rearrange("p (c t) -> p c t", t=128)
                    gB3 = gB[:].rearrange("p (c t) -> p c t", t=128)
                    tt("dve", gi3, gi3, bc(Mc, [4, 16, 128]), ALU.subtract, ["gi", "gM"], ["gi"])
                    act(gi[:], gi[:], AF.Exp, ["gi"], ["gi"])
                    tt("dve", gB3, gB3, bc(Mc, [4, 16, 128]), ALU.add, ["gB", "gM"], ["gB"])
                    act(gB[:], gB[:], AF.Exp, ["gB"], ["gB"], scale=-1.0)
                    Mc2 = gM[:].rearrange("p (c t) -> p c t", t=128)[:, :, 127]
                    memset("dve", gam[:, 0:1], 0.0, ["gam"])
                    tt("dve", gam[:, 1:16], Mc2[:, 0:15], Mc2[:, 1:16], ALU.subtract, ["gM"], ["gam"])
                    act(gam[:, 1:16], gam[:, 1:16], AF.Exp, ["gam"], ["gam"])
                    memset("dve", gam[:, 0:1], 1.0, ["gam"])
                    tt("dve", gR[:], bc(gam[:].unsqueeze(2), [4, 16, 2]), pmask[:], ALU.mult, ["gam", "pmask"], ["gR"])
                    mm(PS[2][:, 0:32], sel4[:], gR[:].rearrange("p c q -> p (c q)"), True, True, ["sel4", "gR"], [pk(2)])
                    cp("dve", gam128[:].rearrange("p c q -> p (c q)"), PS[2][:, 0:32], [pk(2)], ["gam128"])
                    for i in range(NT):
                        tr(PS[3][:, i * 8:i * 8 + 4], gi[0:4, i * 128:(i + 1) * 128], ident_f[0:4, 0:4], ["gi", "ident_f"], [pk(3)])
                        tr(PS[3][:, i * 8 + 4:i * 8 + 8], gB[0:4, i * 128:(i + 1) * 128], ident_f[0:4, 0:4], ["gB", "ident_f"], [pk(3)])
                    cp("dve", tok_sc[:].rearrange("p i e -> p (i e)"), PS[3][:, 0:128], [pk(3)], ["tok_sc"])
                    S.barrier()

                if os.environ.get("MSTOP") == "M1":
                    S.mute = True
                qkT = ME(nc.sbuf_tensor(f"qkT{l}", [128, 4, S_], BF16))
                k_tok = ME(nc.sbuf_tensor(f"k_tok{l}", [128, NT, 256], BF16))
                wM = ME(nc.sbuf_tensor(f"wM{l}", [128, 8, 512], BF16))
                dma("sp", wM[:], wb_in[l][:, 0:512].rearrange("(k p) n -> p k n", p=128), WBIN(l, "M"), ["wM"], chan("wM"))
                with ExitStack() as qes:
                    QE = qes.enter_context
                    qkpre = QE(nc.sbuf_tensor(f"qkpre{l}", [128, 4, 3 + S_], BF16))
                    dgq = QE(nc.sbuf_tensor(f"dgq{l}", [128, 4, 4, 128], BF16))
                    qkw = QE(nc.sbuf_tensor(f"qkw{l}", [128, 4, 4], F32))
                    qkb = QE(nc.sbuf_tensor(f"qkb{l}", [128, 4], F32))
                    dma("sp", qkw[:], qkw_d[l], [], ["qkw"], chan("small"))
                    dma("sp", qkb[:], qkb_d[l], [], ["qkb"], chan("small"))
                    memset("dve", qkpre[:, :, 0:3], 0.0, [("qkpre", -1)])
                    for fc in range(4):
                        for j in range(4):
                            ts("dve", dgq[:, fc, j, :], ident_f[:], qkw[:, fc, j:j + 1], None, ALU.mult, None,
                               ["ident_f", "qkw"], [("dgq", fc)])
                    n = 0
                    for fc in range(4):
                        for c in range(4):
                            pb_ = n % 2; n += 1
                            for k in range(8):
                                mm(PS[pb_][:], wM[:, k, fc * 128:(fc + 1) * 128], xT[:, k, c * 512:(c + 1) * 512], k == 0, k == 7,
                                   ["wM"] + XT_ALL[c * 4:c * 4 + 4], [pk(pb_)])
                            cp("act" if n % 2 else "dve", qkpre[:, fc, 3 + c * 512:3 + (c + 1) * 512], PS[pb_][:], [pk(pb_)], [("qkpre", fc, c)])
                    for fc in range(4):
                        for c in range(4):
                            pb_ = 2 + (n % 2); n += 1
                            rdq = [("qkpre", -1)] + [("qkpre", fc, cc) for cc in range(c + 1)]
                            for j in range(4):
                                mm(PS[pb_][:], dgq[:, fc, j, :], qkpre[:, fc, c * 512 + j:c * 512 + j + 512], j == 0, j == 3,
                                   rdq + [("dgq", fc)], [pk(pb_)])
                            act(qkT[:, fc, c * 512:(c + 1) * 512], PS[pb_][:], AF.Silu, [pk(pb_), "qkb"], [("qkT", fc)], bias=qkb[:, fc:fc + 1])
                    for i in range(NT):
                        pb_ = 4 + (i % 2)
                        for kc in range(2):
                            tr(PSB[pb_][:, kc * 128:(kc + 1) * 128], qkT[:, 2 + kc, i * 128:(i + 1) * 128], ident_b[:],
                               [("qkT", 2 + kc), "ident_b"], [pk(pb_)])
                        cp("dve" if i % 2 else "act", k_tok[:, i, :], PSB[pb_][:, 0:256], [pk(pb_)], [("k_tok", i)])
                    S.barrier()

                if os.environ.get("MSTOP") == "M2":
                    S.mute = True
                vaug = ME(nc.sbuf_tensor(f"vaug{l}", [128, NT, 4, 66], BF16))
                og = ME(nc.sbuf_tensor(f"og{l}", [128, NT, 256], BF16))
                memset("dve", vaug[:, :, :, 64:66], 0.0, [("vaug", i) for i in range(NT)])
                dma("sp", wM[:], wb_in[l][:, 512:1024].rearrange("(k p) n -> p k n", p=128), WBIN(l, "M"), ["wM"], chan("wM"))
                with nc.sbuf_tensor(f"osig{l}", [128, 2, 256], F32) as osig:
                    for i in range(NT):
                        pb_ = i % 2
                        for k in range(8):
                            mm(PS[pb_][:], xT[:, k, i * 128:(i + 1) * 128], wM[:, k, :], k == 0, k == 7, ["wM", ("xT", i)], [pk(pb_)])
                        al = tok_sc[:, i, 0:4]
                        sk = os.environ.get("M3SKIP", "")
                        if "a" not in sk:
                            for h in range(4):
                                ts("dve", vaug[:, i, h, 0:64], PS[pb_][:, h * 64:(h + 1) * 64], tok_sc[:, i, h:h + 1], None, ALU.mult, None,
                                   [pk(pb_), "tok_sc"], [("vaug", i)])
                        if "b" not in sk:
                            cp("dve", vaug[:, i, :, 64:65], al.unsqueeze(2), ["tok_sc"], [("vaug", i)])
                        if "c" not in sk:
                            act(osig[:, pb_, :], PS[pb_][:, 256:512], AF.Sigmoid, [pk(pb_)], [("osig", pb_)])
                        if "d" not in sk:
                            tt("dve", og[:, i, :], osig[:, pb_, :], (g_rep[:, 0:256] if os.environ.get("HGTEST") else hg_rep), ALU.mult, [("osig", pb_), "hg_rep", "g_rep"], [("og", i)])
                    S.barrier()

                if os.environ.get("MSTOP") == "M3":
                    S.mute = True
                ym_tok = ME(nc.sbuf_tensor(f"ym_tok{l}", [128, NT, 256], BF16))
                with ExitStack() as ces2:
                    CE2 = ces2.enter_context
                    St = CE2(nc.sbuf_tensor(f"St{l}", [128, 2, 65], F32))
                    Sg = CE2(nc.sbuf_tensor(f"Sg{l}", [128, 2, 2, 66], BF16))
                    A_bf = CE2(nc.sbuf_tensor(f"A_bf{l}", [128, 2, 4, 128], BF16))
                    hm_t = CE2(nc.sbuf_tensor(f"hm{l}", [128, 3, 4, 64], F32))
                    hsq_t = CE2(nc.sbuf_tensor(f"hsq{l}", [128, 1, 4, 64], F32))
                    st4_t = CE2(nc.sbuf_tensor(f"st4{l}", [128, 3, 8, 4], F32))
                    memset("dve", St[:], 0.0, ["St"])

                    def rec_stage(c):
                        cb = c % 2
                        csl = slice(c * 128, (c + 1) * 128)
                        for pr in range(2):
                            mm(PS[4 + pr][:, 0:264], k_tok[:, c, pr * 128:(pr + 1) * 128],
                               vaug[:, c, :, :].rearrange("p h e -> p (h e)"), True, True, [("k_tok", c), ("vaug", c)], [pk(4 + pr)])
                        for pr in range(2):
                            ts("dve", Sg[:, cb, pr, 0:65], St[:, pr, :], gam128[:, c, pr:pr + 1], 0.125, ALU.mult, ALU.mult,
                               ["St", "gam128"], [("Sg", cb)])
                        for h in range(4):
                            pb0 = (h % 2) * 64
                            pr = h // 2
                            stt(St[pb0:pb0 + 64, pr, :], St[pb0:pb0 + 64, pr, :], gam128[pb0:pb0 + 64, c, pr:pr + 1],
                                PS[4 + pr][pb0:pb0 + 64, h * 66:h * 66 + 65], ALU.mult, ALU.add, ["St", "gam128", pk(4 + pr)], ["St"])
                        for par in range(2):
                            pb0 = par * 64
                            pa = 0 + par
                            for q in range(2):
                                mm(PS[pa][:, q * 128:(q + 1) * 128], qkT[pb0:pb0 + 64, 2 + q, csl], qkT[pb0:pb0 + 64, q, csl], True, True,
                                   [("qkT", 2 + q), ("qkT", q)], [pk(pa)])
                            for q in range(2):
                                tt("dve", A_bf[:, cb, 2 * par + q, :], PS[pa][:, q * 128:(q + 1) * 128], tri125[:], ALU.mult,
                                   [pk(pa), "tri125"], [("A_bf", cb, par)])
                        for par in range(2):
                            pb0 = par * 64
                            pacc = (2 if cb == 0 else 6) + par
                            for q in range(2):
                                h = 2 * q + par
                                mm(PS[pacc][:, q * 65:(q + 1) * 65], A_bf[:, cb, 2 * par + q, :], vaug[:, c, h, 0:65], True, False,
                                   [("A_bf", cb, par), ("vaug", c)], [pk(pacc)])
                                mm(PS[pacc][:, q * 65:(q + 1) * 65], qkT[pb0:pb0 + 64, q, csl], Sg[pb0:pb0 + 64, cb, q, 0:65], False, True,
                                   [("qkT", q), ("Sg", cb)], [pk(pacc)])

                    def post_a(c):
                        cb = c % 2
                        c3 = c % 3
                        hm = hm_t[:, c3, :, 0:64]
                        hsq = hsq_t[:, 0]
                        st4 = st4_t[:, c3]
                        for par in range(2):
                            pacc = (2 if cb == 0 else 6) + par
                            acc3 = PS[pacc][:, 0:130].rearrange("p (h e) -> p h e", h=2)
                            sl = slice(2 * par, 2 * par + 2)
                            act(st4[:, 0, sl], acc3[:, :, 64], AF.Abs, [pk(pacc)], [("st4", c3, 0)])
                            tt("dve", st4[:, 0, sl], st4[:, 0, sl], tok_sc[:, c, 4 + par:8:2], ALU.max, [("st4", c3, 0), "tok_sc"], [("st4", c3, 0)])
                            recip(st4[:, 1, sl], st4[:, 0, sl], [("st4", c3, 0)], [("st4", c3, 1)])
                            for q in range(2):
                                ts("dve", hm[:, 2 * par + q, :], acc3[:, q, 0:64], st4[:, 1, 2 * par + q:2 * par + q + 1], None, ALU.mult, None,
                                   [pk(pacc), ("st4", c3, 1)], [("hm", c3)])
                        treduce(st4[:, 2, :], hm[:, :, :], ALU.add, [("hm", c3)], [("st4", c3, 2)])
                        tt("pool", hsq[:, :, 0:64], hm[:, :, :], hm[:, :, :], ALU.mult, [("hm", c3)], [("hsq", 0)])

                    def post_b(c):
                        c3 = c % 3
                        hsq = hsq_t[:, 0]
                        st4 = st4_t[:, c3]
                        treduce(st4[:, 3, :], hsq[:, :, 0:64], ALU.add, [("hsq", 0)], [("st4", c3, 3)])
                        ts("dve", st4[:, 2, :], st4[:, 2, :], 1.0 / 64, None, ALU.mult, None, [("st4", c3, 2)], [("st4", c3, 2)])
                        tt("dve", st4[:, 4, :], st4[:, 2, :], st4[:, 2, :], ALU.mult, [("st4", c3, 2)], [("st4", c3, 4)])
                        stt(st4[:, 5, :], st4[:, 3, :], 1.0 / 64, st4[:, 4, :], ALU.mult, ALU.subtract, [("st4", c3, 3), ("st4", c3, 4)], [("st4", c3, 5)])
                        act(st4[:, 5, :], st4[:, 5, :], AF.Sqrt, [("st4", c3, 5)], [("st4", c3, 5)], bias=EPS)

                    def post_c(c):
                        c3 = c % 3
                        hm = hm_t[:, c3, :, 0:64]
                        st4 = st4_t[:, c3]
                        recip(st4[:, 6, :], st4[:, 5, :], [("st4", c3, 5)], [("st4", c3, 6)])
                        tt("pool", hm[:, :, :], hm[:, :, :], bc(st4[:, 2, :].unsqueeze(2), [128, 4, 64]), ALU.subtract, [("hm", c3), ("st4", c3, 2)], [("hm", c3)])
                        tt("pool", hm[:, :, :], hm[:, :, :], bc(st4[:, 6, :].unsqueeze(2), [128, 4, 64]), ALU.mult, [("hm", c3), ("st4", c3, 6)], [("hm", c3)])
                        for par in range(2):
                            ogv = og[:, c, :].rearrange("p (h d) -> p h d", h=4)[:, par:4:2, :]
                            ymv = ym_tok[:, c, :].rearrange("p (h d) -> p h d", h=4)[:, par:4:2, :]
                            tt("pool", ymv, hm[:, 2 * par:2 * par + 2, :], ogv, ALU.mult, [("hm", c3), ("og", c)], [("ym_tok", c)])

                    for c in range(NT + 2):
                        if c < NT:
                            rec_stage(c)
                        if 1 <= c <= NT:
                            post_a(c - 1)
                        if c >= 2:
                            post_c(c - 2)
                        if 1 <= c <= NT:
                            post_b(c - 1)
                    for i in range(NT):
                        pb_ = i % 2
                        for kc in range(2):
                            tr(PSB[pb_][:, kc * 128:(kc + 1) * 128], ym_tok[:, i, kc * 128:(kc + 1) * 128], ident_b[:],
                               [("ym_tok", i), "ident_b"], [pk(pb_)])
                        cp("act" if i % 2 else "dve", mixT[:, 0:2, i * 128:(i + 1) * 128],
                           PSB[pb_][:, 0:256].rearrange("p (k t) -> p k t", k=2), [pk(pb_)], [("mixT", 0), ("mixT", 1)])
                    S.barrier()

            S.mute = phases is not None and 'attn' not in phases
            for g in range(2):
                with ExitStack() as aes:
                    AE = aes.enter_context
                    QT = AE(nc.sbuf_tensor(f"QT{l}{g}", [72, 4, S_], BF16))
                    KT = AE(nc.sbuf_tensor(f"KT{l}{g}", [72, 4, S_], BF16))
                    Vaug = AE(nc.sbuf_tensor(f"Vaug{l}{g}", [128, NT, 4, 66], BF16))
                    kmh = AE(nc.sbuf_tensor(f"kmh{l}{g}", [64, 4, 16], BF16))
                    for hh in range(4):
                        dma("sp", KT[64:72, hh, :], oneh_d, [], [("KT", hh)], chan("small"))
                    memset("dve", Vaug[:, :, :, 64:65], 1.0, [("Vaug", i) for i in range(NT)])
                    with ExitStack() as a1:
                        A1 = a1.enter_context
                        wA = A1(nc.sbuf_tensor(f"wA{l}{g}", [128, 8, 768], BF16))
                        qk_t = A1(nc.sbuf_tensor(f"qk_t{l}{g}", [128, 2, 512], BF16))
                        km_f = A1(nc.sbuf_tensor(f"km_f{l}{g}", [128, 2, 8], F32))
                        kmT2 = A1(nc.sbuf_tensor(f"kmT2{l}{g}", [128, 2, 16], BF16))
                        km_t = A1(nc.sbuf_tensor(f"km_t{l}{g}", [128, 2, 8], F32))
                        for part, c0 in enumerate((1544, 2056, 2568)):
                            dma("sp", wA[:, :, part * 256:(part + 1) * 256],
                                wb_in[l][:, c0 + g * 256:c0 + (g + 1) * 256].rearrange("(k p) n -> p k n", p=128), WBIN(l, "A"), ["wA"], chan("wA"))
                        def a1_proj(i):
                            b = i % 2
                            pq, pv = 0 + b, 2 + b
                            for k in range(8):
                                mm(PS[pq][:], xT[:, k, i * 128:(i + 1) * 128], wA[:, k, 0:512], k == 0, k == 7, ["wA", ("xT", i)], [pk(pq)])
                            for k in range(8):
                                mm(PS[pv][:, 0:256], xT[:, k, i * 128:(i + 1) * 128], wA[:, k, 512:768], k == 0, k == 7, ["wA", ("xT", i)], [pk(pv)])
                            act(qk_t[:, b, 0:256], PS[pq][:, 0:256], AF.Copy, [pk(pq)], [("qk_t", b)], scale=0.125)
                            cp("dve", qk_t[:, b, 256:512], PS[pq][:, 256:512], [pk(pq)], [("qk_t", b)])
                            cp("act", Vaug[:, i, :, 0:64], PS[pv][:, 0:256].rearrange("p (h d) -> p h d", h=4), [pk(pv)], [("Vaug", i)])

                        def a1_tr(i):
                            b = i % 2
                            ptq = 4 + b
                            for hh in range(4):
                                tr(PSB[ptq][0:64, hh * 128:(hh + 1) * 128], qk_t[:, b, hh * 64:(hh + 1) * 64], ident_b[:],
                                   [("qk_t", b), "ident_b"], [pk(ptq)])
                                tr(PSB[ptq][0:64, 512 + hh * 128:512 + (hh + 1) * 128], qk_t[:, b, 256 + hh * 64:256 + (hh + 1) * 64], ident_b[:],
                                   [("qk_t", b), "ident_b"], [pk(ptq)])
                            cp("act", QT[0:64, :, i * 128:(i + 1) * 128], PSB[ptq][0:64, 0:512].rearrange("p (h t) -> p h t", h=4),
                               [pk(ptq)], [("QT", hh) for hh in range(4)])
                            cp("dve", KT[0:64, :, i * 128:(i + 1) * 128], PSB[ptq][0:64, 512:1024].rearrange("p (h t) -> p h t", h=4),
                               [pk(ptq)], [("KT", hh) for hh in range(4)])
                            nb = i // 2
                            for pr in range(2):
                                mm(PS[6 + pr][:, nb * 2:nb * 2 + 2], qk_t[:, b, 256 + pr * 128:256 + (pr + 1) * 128], onecol[:],
                                   i % 2 == 0, i % 2 == 1, [("qk_t", b), "onecol"], [pk(6 + pr)])

                        a1_proj(0)
                        for i in range(NT):
                            if i + 1 < NT:
                                a1_proj(i + 1)
                            a1_tr(i)
                        for pr in range(2):
                            cp("dve", km_f[:, pr, :], PS[6 + pr][:, 0:16].rearrange("p (n two) -> p n two", two=2)[:, :, 0], [pk(6 + pr)], ["km_f"])
                        cp("dve", kmT2[:, :, 0:8], km_f[:], ["km_f"], ["kmT2"])
                        cp("dve", km_t[:], kmT2[:, :, 0:8], ["kmT2"], ["km_t"])
                        tt("dve", kmT2[:, :, 8:16], km_f[:], km_t[:], ALU.subtract, ["km_f", "km_t"], ["kmT2"])
                        for hh in range(4):
                            pb0 = (hh % 2) * 64
                            cp("dve", kmh[:, hh, :], kmT2[pb0:pb0 + 64, hh // 2, :], ["kmT2"], ["kmh"])
                        S.barrier()

                    with ExitStack() as a2:
                        A2 = a2.enter_context
                        gsb = A2(nc.sbuf_tensor(f"gsb{l}{g}", [128, NT, 4, 16], F32))
                        gate = A2(nc.sbuf_tensor(f"gate{l}{g}", [128, NT * 4, 8], F32))
                        cmpt = A2(nc.sbuf_tensor(f"cmpt{l}{g}", [128, NT * 4, 8, 8], BF16))
                        rank = A2(nc.sbuf_tensor(f"rank{l}{g}", [128, NT * 4, 8], F32))
                        mval = A2(nc.sbuf_tensor(f"mval{l}{g}", [128, NT, 4, 8], BF16))
                        for i in range(NT):
                            b = i // 8
                            for hh in range(4):
                                co = (i % 8) * 64 + hh * 16
                                mm(PS[b][:, co:co + 16], QT[0:64, hh, i * 128:(i + 1) * 128], kmh[:, hh, :], True, True,
                                   [("QT", hh), "kmh"], [pk(b)])
                        for b in range(2):
                            cp("act", gsb[:, b * 8:(b + 1) * 8].rearrange("p i h e -> p (i h e)"), PS[b][:, 0:512], [pk(b)], ["gsb"])
                        gsbv = gsb[:].rearrange("p i h e -> p (i h) e")
                        tt("dve", gate[:], gsbv[:, :, 0:8], gsbv[:, :, 8:16], ALU.add, ["gsb"], ["gate"])
                        for nb in range(8):
                            memset("dve", gate[:, nb * 8:(nb + 1) * 8, nb:8], NEG, ["gate"])
                        tt("dve", cmpt[:], bc(gate[:].unsqueeze(3), [128, NT * 4, 8, 8]), bc(gate[:].unsqueeze(2), [128, NT * 4, 8, 8]), ALU.is_lt,
                           ["gate"], ["cmpt"])
                        treduce(rank[:], cmpt[:], ALU.add, ["cmpt"], ["rank"])
                        ts("dve", mval[:].rearrange("p i h e -> p (i h) e"), rank[:], 3.0, NEG, ALU.is_ge, ALU.mult, ["rank"], [("mval", i) for i in range(NT)])
                        for nb in range(8):
                            memset("dve", mval[:, 2 * nb:2 * nb + 2, :, nb:nb + 1], 0.0, [("mval", 2 * nb), ("mval", 2 * nb + 1)])
                        tt("dve", mval[:], mval[:], bc(b31_sb[:, 4 * g:4 * g + 4].unsqueeze(1).unsqueeze(3), [128, NT, 4, 8]), ALU.add,
                           [("mval", i) for i in range(NT)] + ["b31"], [("mval", i) for i in range(NT)])
                        for hh in range(4):
                            for half in range(2):
                                pm_ = 2 + (hh * 2 + half) % 2
                                for ii in range(8):
                                    i = half * 8 + ii
                                    tr(PSB[pm_][0:8, ii * 128:(ii + 1) * 128], mval[:, i, hh, :], ident_b[:], [("mval", i), "ident_b"], [pk(pm_)])
                                cp("act" if half else "dve", QT[64:72, hh, half * 1024:(half + 1) * 1024], PSB[pm_][0:8, :], [pk(pm_)], [("QT", hh)])
                        S.barrier()

                    with ExitStack() as a3:
                        A3 = a3.enter_context
                        PT = A3(nc.sbuf_tensor(f"PT{l}{g}", [128, 3, 512], BF16))
                        tmpn = A3(nc.sbuf_tensor(f"tmpn{l}{g}", [128, 3, 256], F32))
                        ya_tok = A3(nc.sbuf_tensor(f"ya_tok{l}{g}", [128, NT, 256], BF16))
                        rden = A3(nc.sbuf_tensor(f"rden{l}{g}", [128, 4], F32))
                        iters = [(hh, c, j) for hh in range(4) for c in range(4) for j in range(4 * c + 4)]
                        SBANK = [0, 1, 6]

                        def st_stage(n):
                            hh, c, j = iters[n]
                            b = n % 3
                            c0 = max(0, j - 4 * c) * 128
                            mm(PS[SBANK[b]][:, c0:512], KT[0:72, hh, j * 128:(j + 1) * 128], QT[0:72, hh, c * 512 + c0:(c + 1) * 512], True, True,
                               [("KT", hh), ("QT", hh)], [pk(SBANK[b])])

                        def pv_stage(n):
                            hh, c, j = iters[n]
                            hg_ = g * 4 + hh
                            b = n % 3
                            psb_ = SBANK[b]
                            tt0 = max(0, j - 4 * c)
                            c0 = tt0 * 128
                            n0 = max(c0, 128 * j - 512 * c)
                            n1 = min(512, 128 * j - 512 * c + 256)
                            if n1 > n0:
                                u0 = 512 * c + n0 - 128 * j
                                tt("dve", PS[psb_][:, n0:n1], PS[psb_][:, n0:n1], G_sb[:, hg_, u0:u0 + (n1 - n0)], ALU.add,
                                   [pk(psb_), "G_sb"], [pk(psb_)])
                            act(PT[:, b, c0:512], PS[psb_][:, c0:512], AF.Exp, [pk(psb_)], [("PT", b)])
                            for tti in range(tt0, 4):
                                pacc = 2 + tti
                                mm(PS[pacc][:, 0:65], PT[:, b, tti * 128:(tti + 1) * 128], Vaug[:, j, hh, 0:65], j == 0, j == 4 * c + tti,
                                   [("PT", b), ("Vaug", j)], [pk(pacc)])
                            if j == 4 * c + 3:
                                for tti in range(4):
                                    pacc = 2 + tti
                                    i = 4 * c + tti
                                    recip(rden[:, tti:tti + 1], PS[pacc][:, 64:65], [pk(pacc)], [("rden", tti)])
                                    ts("dve", ya_tok[:, i, hh * 64:(hh + 1) * 64], PS[pacc][:, 0:64], rden[:, tti:tti + 1], None, ALU.mult, None,
                                       [pk(pacc), ("rden", tti)], [("ya_tok", i)])

                        NIT = len(iters)
                        st_stage(0)
                        st_stage(1)
                        for n in range(NIT):
                            if n + 2 < NIT:
                                st_stage(n + 2)
                            pv_stage(n)
                        for i in range(NT):
                            pb_ = 7 if i % 2 else 0
                            for kc in range(2):
                                tr(PSB[pb_][:, kc * 128:(kc + 1) * 128], ya_tok[:, i, kc * 128:(kc + 1) * 128], ident_b[:],
                                   [("ya_tok", i), "ident_b"], [pk(pb_)])
                            cp("act" if i % 2 else "dve", mixT[:, 4 + 2 * g:6 + 2 * g, i * 128:(i + 1) * 128],
                               PSB[pb_][:, 0:256].rearrange("p (k t) -> p k t", k=2), [pk(pb_)], [("mixT", 4 + 2 * g), ("mixT", 5 + 2 * g)])
                        S.barrier()

            S.mute = False
            if debug == "mix" and l == 0:
                dma("sp", dbg_d, mixT[:], [("mixT", k) for k in range(8)], ["dbg"], chan("out"))
                S.barrier()

            S.mute = phases is not None and 'out' not in phases
            with nc.sbuf_tensor(f"wO{l}", [128, 8, D_], BF16) as wO:
                dma("sp", wO[:], wb_out[l].rearrange("(k p) n -> p k n", p=128), WBOUT(l), ["wO"], chan("wO"))
                for i in range(NT):
                    for half in range(2):
                        pb_ = (i * 2 + half) % 4
                        for k in range(8):
                            mm(PS[pb_][:], mixT[:, k, i * 128:(i + 1) * 128], wO[:, k, half * 512:(half + 1) * 512], k == 0, k == 7,
                               ["wO", ("mixT", k)], [pk(pb_)])
                        tt("dve", x_sb[:, i, half * 512:(half + 1) * 512], x_sb[:, i, half * 512:(half + 1) * 512], PS[pb_][:], ALU.add,
                           [pk(pb_), ("x", i)], [("x", i)])
                if phases is None or 'ffn' in phases:
                    norm_to_xT(norm2_d[l, :], f"n2_{l}", pbase=4)
                S.barrier()

        S.mute = False
        if debug == "x1" and l == 0:
            dma("sp", dbg_d.rearrange("(i p) d -> p i d", p=128), x_sb[:], [("x", i) for i in range(NT)], ["dbg"], chan("out"))

        S.mute = phases is not None and 'ffn' not in phases
        with ExitStack() as fes:
            FE = fes.enter_context
            actT = FE(nc.sbuf_tensor(f"actT{l}", [128, NJ, 512], BF16))
            wU = FE(nc.sbuf_tensor(f"wU{l}", [128, 2, 2, 8, 256], BF16))
            wD = FE(nc.sbuf_tensor(f"wD{l}", [128, 2, NJ, 512], BF16))
            upre = FE(nc.sbuf_tensor(f"upre{l}", [128, 2, 2, 514], BF16))
            halo = FE(nc.sbuf_tensor(f"halo{l}", [128, 2 * NJ, 2], BF16))
            dgf = FE(nc.sbuf_tensor(f"dgf{l}", [128, 2, 2, 3, 128], BF16))
            fcw = FE(nc.sbuf_tensor(f"fcw{l}", [128, 2 * NJ, 3], F32))
            fcb = FE(nc.sbuf_tensor(f"fcb{l}", [128, 2 * NJ], F32))
            sgf = FE(nc.sbuf_tensor(f"sgf{l}", [128, 2, 512], F32))
            dma("sp", fcw[:], fcw_d[l], [], ["fcw"], chan("small"))
            dma("sp", fcb[:], fcb_d[l], [], ["fcb"], chan("small"))
            memset("dve", halo[:], 0.0, [("halo", jx) for jx in range(NJ)])
            halo4 = halo[:].rearrange("p (h j) e -> p h j e", h=2)
            fcw4 = fcw[:].rearrange("p (h j) e -> p h j e", h=2)
            for half in range(2):
                dma("sp", wD[:, half, 0:21, :], wb_dn[l][0:2688, half * 512:(half + 1) * 512].rearrange("(j p) n -> p j n", p=128),
                    WBDN(l), [("wD", half)], chan("wD"))
                dma("sp", wD[0:64, half, 21, :], wb_dn[l][2688:2752, half * 512:(half + 1) * 512], WBDN(l), [("wD", half)], chan("wD"))
            for c in range(4):
                def up_stage(j):
                    j2 = j // 2
                    jj = j % 2
                    wb_ = j2 % 2
                    ub = j % 2
                    wj = 128 if j < 21 else 64
                    if jj == 0:
                        ncols = 256 if j2 < 10 else 192
                        for half in range(2):
                            col0 = half * DFF + j2 * 256
                            dma("sp", wU[:, wb_, half, :, 0:ncols], wb_up[l][:, col0:col0 + ncols].rearrange("(k p) n -> p k n", p=128),
                                WBUP(l), [("wU", wb_)], chan(f"wU{wb_}"))
                    tt("dve", dgf[0:wj, ub, :, :, 0:wj],
                       bc(ident_f[0:wj, 0:wj].unsqueeze(1).unsqueeze(1), [wj, 2, 3, wj]),
                       bc(fcw4[0:wj, :, j, :].unsqueeze(3), [wj, 2, 3, wj]), ALU.mult,
                       ["ident_f", "fcw"], [("dgf", ub, 0), ("dgf", ub, 1)])
                    for half in range(2):
                        pu = half + 2 * ub
                        for k in range(8):
                            mm(PS[pu][0:wj, :], wU[:, wb_, half, k, jj * 128:jj * 128 + wj], xT[:, k, c * 512:(c + 1) * 512], k == 0, k == 7,
                               [("wU", wb_)] + XT_ALL[c * 4:c * 4 + 4], [pk(pu)])
                        cp("act", upre[0:wj, ub, half, 2:514], PS[pu][0:wj, :], [pk(pu)], [("upre", ub, half)])
                    cp("dve", upre[0:wj, ub, :, 0:2], halo4[0:wj, :, j, :], [("halo", j)], [("upre", ub, 0), ("upre", ub, 1)])
                    cp("dve", halo4[0:wj, :, j, :], upre[0:wj, ub, :, 512:514], [("upre", ub, 0), ("upre", ub, 1)], [("halo", j)])

                def conv_stage(j):
                    ub = j % 2
                    wj = 128 if j < 21 else 64
                    for half in range(2):
                        pcv = 4 + half + 2 * ub
                        for tap in range(3):
                            mm(PS[pcv][0:wj, :], dgf[0:wj, ub, half, tap, 0:wj], upre[0:wj, ub, half, tap:tap + 512], tap == 0, tap == 2,
                               [("dgf", ub, half), ("upre", ub, half)], [pk(pcv)])
                    act(sgf[0:wj, ub, :], PS[4 + 2 * ub][0:wj, :], AF.Silu, [pk(4 + 2 * ub), "fcb"], [("sgf", ub)], bias=fcb[0:wj, j:j + 1])
                    stt(actT[0:wj, j, :], PS[5 + 2 * ub][0:wj, :], fcb[0:wj, NJ + j:NJ + j + 1], sgf[0:wj, ub, :], ALU.add, ALU.mult,
                        [pk(5 + 2 * ub), "fcb", ("sgf", ub)], [("actT", j)])

                up_stage(0)
                for j in range(NJ):
                    if j + 1 < NJ:
                        up_stage(j + 1)
                    conv_stage(j)
                for tti in range(4):
                    i = 4 * c + tti
                    for half in range(2):
                        pd_ = half
                        for j in range(NJ):
                            wj = 128 if j < 21 else 64
                            mm(PS[pd_][:], actT[0:wj, j, tti * 128:(tti + 1) * 128], wD[0:wj, half, j, :], j == 0, j == NJ - 1,
                               [("actT", j), ("wD", half)], [pk(pd_)])
                        tt("dve", x_sb[:, i, half * 512:(half + 1) * 512], x_sb[:, i, half * 512:(half + 1) * 512], PS[pd_][:], ALU.add,
                           [pk(pd_), ("x", i)], [("x", i)])
            S.barrier()
        S.mute = False
        if debug == "x2" and l == 0:
            dma("sp", dbg_d.rearrange("(i p) d -> p i d", p=128), x_sb[:], [("x", i) for i in range(NT)], ["dbg"], chan("out"))

    dma("sp", g_rep[:], final_d[0, :].partition_broadcast(128), [], ["g_rep"], chan("grep"))
    with nc.sbuf_tensor("sqj_f", [128, D_], BF16) as sqj, nc.sbuf_tensor("yo", [128, 2, D_], F32) as yo:
        for i in range(NT):
            b = i % 2
            act(sqj[:], x_sb[:, i, :], AF.Square, [("x", i)], [("ss", i)], accum_out=ss[:, i:i + 1])
            ts("dve", rstd[:, i:i + 1], ss[:, i:i + 1], 1.0 / D_, EPS, ALU.mult, ALU.add, [("ss", i)], [("rstd", i)])
            act(rstd[:, i:i + 1], rstd[:, i:i + 1], AF.Sqrt, [("rstd", i)], [("rstd", i)])
            recip(rstd[:, i:i + 1], rstd[:, i:i + 1], [("rstd", i)], [("rstd", i)])
            stt(yo[:, b, :], x_sb[:, i, :], rstd[:, i:i + 1], g_rep[:], ALU.mult, ALU.mult, [("x", i), ("rstd", i), "g_rep"], [("yo", b)])
            dma("sp", out_d[i * 128:(i + 1) * 128, :], yo[:, b, :], [("yo", b)], [("out", i)], chan(f"out{b}"))
        fin_reads = [("out", i) for i in range(NT)] + (["dbg"] if dbg_d is not None else [])
        S.op("sp", None, reads=fin_reads, writes=[])
        S.barrier()
    chan_sems = {v: chan_sem_list[int(v[1:])] for v in chan_ids.values()}
    S.emit(sems, chan_sems, block)
    es.close()
    return nc


def make_in_maps(inputs):
    f = lambda a: np.ascontiguousarray(np.asarray(a, dtype=np.float32))
    c = host_constants()
    G, b31 = t5_tables(f(inputs["rel_bias"]))
    fcw = f(inputs["ffn_conv_w"])
    fcb = f(inputs["ffn_conv_b"])
    fcw_p = np.zeros((2, 2, NJ * 128, 3), np.float32)
    fcb_p = np.zeros((2, 2, NJ * 128), np.float32)
    for half in range(2):
        fcw_p[:, half, :DFF, :] = np.transpose(fcw[:, :, half * DFF:(half + 1) * DFF], (0, 2, 1))
        fcb_p[:, half, :DFF] = fcb[:, half * DFF:(half + 1) * DFF]
    pc = lambda a, nchunk: np.ascontiguousarray(np.moveaxis(a.reshape((2, nchunk, 128) + a.shape[2:]), 1, 2))
    qkw_h = pc(np.transpose(f(inputs["mlstm_qk_conv_w"]), (0, 2, 1)), 4)
    qkb_h = pc(f(inputs["mlstm_qk_conv_b"]), 4)
    cfw_h = pc(np.transpose(f(inputs["conf_dw_w"]), (0, 2, 1)), 2)
    cvec_h = np.ascontiguousarray(np.stack([pc(f(inputs["conf_dw_b"]), 2), pc(f(inputs["conf_ln_g"]), 2),
                                            pc(f(inputs["conf_ln_b"]), 2)], axis=2))
    fcw_h = pc(fcw_p.reshape(2, 2 * NJ * 128, 3), 2 * NJ)
    fcb_h = pc(fcb_p.reshape(2, 2 * NJ * 128), 2 * NJ)
    shared = {
        "w_in": f(inputs["w_in"]), "w_out": f(inputs["w_out"]),
        "ffn_w_up": f(inputs["ffn_w_up"]), "ffn_w_down": f(inputs["ffn_w_down"]),
        "norm1_g": f(inputs["norm1_g"]), "norm2_g": f(inputs["norm2_g"]),
        "final_g": f(inputs["final_g"]).reshape(1, D_),
        "qkw": qkw_h, "qkb": qkb_h,
        "igb": f(inputs["mlstm_ig_b"]).reshape(2, 4, 1),
        "fgb": f(inputs["mlstm_fg_b"]).reshape(2, 4, 1),
        "headg": f(inputs["mlstm_head_g"]),
        "cfw": cfw_h, "cvec": cvec_h, "fcw": fcw_h, "fcb": fcb_h,
        "gt5": G, "b31": b31,
        "ident": c["ident"], "tri125": c["tri125"], "blk_onehot": c["blk_onehot"],
        "sel4": c["sel4"], "pairmask": c["pairmask"],
    }
    x = f(inputs["x"])
    return [dict(shared, x=np.ascontiguousarray(x[b])) for b in range(x.shape[0])]


_NC_CACHE = {}


def kernel(**inputs):
    in_maps = make_in_maps(inputs)
    if "nc" not in _NC_CACHE:
        _NC_CACHE["nc"] = build(2)
    res = run_bass_kernel_spmd(_NC_CACHE["nc"], in_maps, core_ids=list(range(8)))
    return np.stack([r["out"] for r in res.results], axis=0).astype(np.float32)
```

```python
import math
import os
from contextlib import ExitStack
import numpy as np
import ml_dtypes
import concourse.bass as bass
import concourse.mybir as mybir
from concourse.bass_utils import run_bass_kernel_spmd

F32 = mybir.dt.float32
BF16 = mybir.dt.bfloat16
AF = mybir.ActivationFunctionType
ALU = mybir.AluOpType
AX = mybir.AxisListType

S_ = 2048
D_ = 1024
NT = 16
P_IN = 3080
DFF = 2752
NEG = -1.0e30
EPS = 1e-6
NJ = 22


class Sched:
    ENGS = ("pe", "act", "dve", "pool", "sp")

    def __init__(self, nc):
        self.nc = nc
        self.instrs = []
        self.streams = {e: [] for e in self.ENGS}
        self.last_w = {}
        self.readers = {}
        self.seen = {e: {f: -1 for f in self.ENGS} for e in self.ENGS}
        self.seen_dma = {e: {} for e in self.ENGS}
        self.chan_count = {}
        self.chans = []
        self.no_barrier = set()
        self.mute = False
        self.psx_last = {}

    def op(self, eng, fn, reads=(), writes=(), chan=None, extra_deps=(), exact=False):
        if self.mute:
            return -1
        idx = len(self.instrs)
        deps = set(extra_deps)
        for r in reads:
            if isinstance(r, tuple) and len(r) == 2 and r[0] == "ps":
                last = self.psx_last.get(r[1])
                if last is not None and self.instrs[last]["eng"] != eng:
                    deps.add(last)
                self.psx_last[r[1]] = idx
        for r in reads:
            w = self.last_w.get(r)
            if w is not None:
                deps.add(w)
        for w_ in writes:
            w = self.last_w.get(w_)
            if w is not None:
                deps.add(w)
            for r in self.readers.get(w_, ()):
                deps.add(r)
        rec = dict(idx=idx, eng=eng, fn=fn, waits=[], signal=False, chan=chan,
                   pos=len(self.streams[eng]), exact=exact)
        if chan is not None and chan not in self.chan_count:
            self.chan_count[chan] = 0
            self.chans.append(chan)
        for d in sorted(deps):
            J = self.instrs[d]
            if J["chan"] is not None:
                c = J["chan"]
                val = J["dma_val"] if J["exact"] else self.chan_count[c]
                if self.seen_dma[eng].get(c, 0) >= val:
                    continue
                self.seen_dma[eng][c] = val
                rec["waits"].append(("dma", c, val))
            else:
                F = J["eng"]
                if F == "pe" and eng == "pe":
                    continue
                if self.seen[eng][F] >= J["pos"]:
                    continue
                self.seen[eng][F] = J["pos"]
                J["signal"] = True
                rec["waits"].append(("eng", F, d))
        if chan is not None:
            self.chan_count[chan] += 16
            rec["dma_val"] = self.chan_count[chan]
        self.instrs.append(rec)
        self.streams[eng].append(idx)
        for r in reads:
            self.readers.setdefault(r, []).append(idx)
        for w_ in writes:
            self.last_w[w_] = idx
            self.readers[w_] = []
        return idx

    def barrier(self):
        if self.mute:
            return
        lasts = []
        for e in self.ENGS:
            for idx in reversed(self.streams[e]):
                if self.instrs[idx]["fn"] is not None:
                    lasts.append(idx)
                    break
        chan_state = {c: v for c, v in self.chan_count.items() if c not in self.no_barrier}
        for e in self.ENGS:
            if e == "sp" and not self.streams[e]:
                continue
            idx = self.op(e, None, extra_deps=[l for l in lasts if self.instrs[l]["chan"] is None])
            rec = self.instrs[idx]
            for c, v in chan_state.items():
                if v > 0 and self.seen_dma[e].get(c, 0) < v:
                    self.seen_dma[e][c] = v
                    rec["waits"].append(("dma", c, v))

    def emit(self, sems, chan_sems, block):
        for e in self.ENGS:
            c = 0
            for idx in self.streams[e]:
                r = self.instrs[idx]
                if r["chan"] is None and r["signal"]:
                    c += 1
                r["count"] = c
        instrs = self.instrs

        def run_stream(e):
            def body(engine):
                waited = {}
                pending_sig = 0
                for idx in self.streams[e]:
                    r = instrs[idx]
                    need = {}
                    for w in r["waits"]:
                        if w[0] == "dma":
                            key = ("c", w[1]); val = w[2]
                        else:
                            key = ("e", w[1]); val = instrs[w[2]]["count"]
                        need[key] = max(need.get(key, 0), val)
                    for key, val in need.items():
                        if waited.get(key, 0) >= val:
                            continue
                        waited[key] = val
                        s = chan_sems[key[1]] if key[0] == "c" else sems[key[1]]
                        engine.wait_ge(s, val)
                    if r["fn"] is None:
                        assert not r["signal"]
                        continue
                    ins = r["fn"](engine)
                    if r["chan"] is not None:
                        ins.then_inc(chan_sems[r["chan"]], 16)
                    elif r["signal"]:
                        ins.then_inc(sems[e], 1)
            return body

        deco = {"pe": block.tensor, "act": block.scalar, "dve": block.vector,
                "pool": block.gpsimd, "sp": block.sync}
        for e in self.ENGS:
            if self.streams[e]:
                deco[e](run_stream(e))


def _t5_bucket_np(dist):
    n = np.maximum(dist, 0)
    max_exact = 16
    nf = np.maximum(n, 1).astype(np.float32)
    large = max_exact + (np.log(nf / max_exact) / math.log(128 / max_exact) * (32 - max_exact)).astype(np.int32)
    large = np.minimum(large, 31)
    return np.where(n < max_exact, n, large)


GU = 384


def host_constants():
    c = {}
    c["ident"] = np.eye(128, dtype=np.float32)
    s = np.arange(128)[:, None]
    t = np.arange(128)[None, :]
    c["tri125"] = np.where(s <= t, 0.125, 0.0).astype(np.float32)
    oh = np.zeros((8, S_), dtype=np.float32)
    for n in range(8):
        oh[n, n * 256:(n + 1) * 256] = 1.0
    c["blk_onehot"] = oh.astype(ml_dtypes.bfloat16)
    sel = np.zeros((4, 128), dtype=np.float32)
    for h in range(4):
        sel[h, (h % 2) * 64:(h % 2) * 64 + 64] = 1.0
    c["sel4"] = sel
    dl = np.zeros((4, 16, 2), dtype=np.float32)
    for h in range(4):
        dl[h, :, h // 2] = 1.0
    c["pairmask"] = dl
    return c


def t5_tables(rel_bias):
    p = np.arange(128)[:, None]
    u = np.arange(GU)[None, :]
    dist = u - p
    b = _t5_bucket_np(dist)
    G = rel_bias[b]
    G = np.where((dist >= 0)[:, :, None], G, np.float32(NEG))
    G = np.ascontiguousarray(np.transpose(G, (0, 2, 1))).astype(np.float32)
    b31 = np.ascontiguousarray(np.broadcast_to(rel_bias[31][None, :], (128, 8))).astype(np.float32)
    return G, b31


def build(n_layers=2, debug=None, phases=None):
    nc = bass.Bass("TRN2", target_bir_lowering=False)
    L = n_layers

    def din(name, shape, dt=F32):
        return nc.dram_tensor(name, list(shape), dt, kind="ExternalInput").ap()

    x_d = din("x", [S_, D_])
    w_in_d = din("w_in", [2, D_, P_IN])
    w_out_d = din("w_out", [2, D_, D_])
    w_up_d = din("ffn_w_up", [2, D_, 2 * DFF])
    w_dn_d = din("ffn_w_down", [2, DFF, D_])
    norm1_d = din("norm1_g", [2, D_])
    norm2_d = din("norm2_g", [2, D_])
    final_d = din("final_g", [1, D_])
    qkw_d = din("qkw", [2, 128, 4, 4])
    qkb_d = din("qkb", [2, 128, 4])
    igb_d = din("igb", [2, 4, 1])
    fgb_d = din("fgb", [2, 4, 1])
    hg_d = din("headg", [2, 256])
    cw_d = din("cfw", [2, 128, 2, 31])
    cvec_d = din("cvec", [2, 128, 3, 2])
    fcw_d = din("fcw", [2, 128, 2 * NJ, 3])
    fcb_d = din("fcb", [2, 128, 2 * NJ])
    G_d = din("gt5", [128, 8, GU])
    b31_d = din("b31", [128, 8])
    ident_d = din("ident", [128, 128])
    tri_d = din("tri125", [128, 128])
    oneh_d = din("blk_onehot", [8, S_], BF16)
    sel4_d = din("sel4", [4, 128])
    pm_d = din("pairmask", [4, 16, 2])
    out_d = nc.dram_tensor("out", [S_, D_], F32, kind="ExternalOutput").ap()
    dbg_d = None
    if debug == "mix":
        dbg_d = nc.dram_tensor("dbg", [128, 8, S_], BF16, kind="ExternalOutput").ap()
    if debug in ("x1", "x2"):
        dbg_d = nc.dram_tensor("dbg", [S_, D_], F32, kind="ExternalOutput").ap()

    wb_in = [nc.dram_tensor(f"wb_in{l}", [D_, P_IN], BF16, kind="Internal").ap() for l in range(L)]
    wb_out = [nc.dram_tensor(f"wb_out{l}", [D_, D_], BF16, kind="Internal").ap() for l in range(L)]
    wb_up = [nc.dram_tensor(f"wb_up{l}", [D_, 2 * DFF], BF16, kind="Internal").ap() for l in range(L)]
    wb_dn = [nc.dram_tensor(f"wb_dn{l}", [DFF, D_], BF16, kind="Internal").ap() for l in range(L)]

    S = Sched(nc)
    es = ExitStack()
    E = es.enter_context
    sb = lambda name, shape, dt: E(nc.sbuf_tensor("sb_" + name, list(shape), dt))

    x_sb = sb("x_sb", [128, NT, D_], F32)
    xT = sb("xT", [128, 8, S_], BF16)
    ident_f = sb("ident_f", [128, 128], F32)
    ident_b = sb("ident_b", [128, 128], BF16)
    ones_f = sb("ones_f", [128, 128], F32)
    tri125 = sb("tri125", [128, 128], F32)
    G_sb = sb("G_sb", [128, 8, GU], BF16)
    b31_sb = sb("b31_sb", [128, 8], F32)
    g_rep = sb("g_rep", [128, D_], F32)
    ss = sb("ss", [128, NT], F32)
    rstd = sb("rstd", [128, NT], F32)
    onecol = sb("onecol", [128, 2], BF16)
    sel4 = sb("sel4", [4, 128], F32)
    pmask = sb("pmask", [4, 16, 2], F32)
    PS = [E(nc.psum_tensor(f"ps{i}", [128, 512], F32)) for i in range(8)]
    PSB = [p.bitcast(BF16) for p in PS]
    sems = {e: E(nc.semaphore(f"s_{e}")) for e in Sched.ENGS}
    NCHAN = 40
    chan_sem_list = [E(nc.semaphore(f"c{i}")) for i in range(NCHAN)]
    block = E(nc.Block())

    pk = lambda i: ("ps", i)

    def dma(q, out, in_, R, W, chan, **kw):
        return S.op(q, lambda e: e.dma_start(out=out, in_=in_, **kw), reads=R, writes=W, chan=chan)

    def mm(out, lhsT, rhs, start, stop, R, W):
        return S.op("pe", lambda e: e.matmul(out, lhsT=lhsT, rhs=rhs, start=start, stop=stop), reads=R, writes=W)

    def tr(out, in_, ident, R, W):
        return S.op("pe", lambda e: e.transpose(out, in_, ident), reads=R, writes=W)

    def act(out, in_, func, R, W, bias=0.0, scale=1.0, accum_out=None):
        if accum_out is None:
            return S.op("act", lambda e: e.activation(out=out, in_=in_, func=func, bias=bias, scale=scale), reads=R, writes=W)
        return S.op("act", lambda e: e.activation(out=out, in_=in_, func=func, bias=bias, scale=scale, accum_out=accum_out), reads=R, writes=W)

    def tt(eng, out, in0, in1, op, R, W):
        return S.op(eng, lambda e: e.tensor_tensor(out=out, in0=in0, in1=in1, op=op), reads=R, writes=W)

    def ts(eng, out, in0, s1, s2, op0, op1, R, W):
        if s2 is None:
            return S.op(eng, lambda e: e.tensor_scalar(out=out, in0=in0, scalar1=s1, scalar2=None, op0=op0), reads=R, writes=W)
        return S.op(eng, lambda e: e.tensor_scalar(out=out, in0=in0, scalar1=s1, scalar2=s2, op0=op0, op1=op1), reads=R, writes=W)

    def stt(out, in0, scalar, in1, op0, op1, R, W):
        return S.op("dve", lambda e: e.scalar_tensor_tensor(out=out, in0=in0, scalar=scalar, in1=in1, op0=op0, op1=op1), reads=R, writes=W)

    def cp(eng, out, in_, R, W):
        if eng == "act":
            return act(out, in_, AF.Copy, R, W)
        return S.op(eng, lambda e: e.tensor_copy(out=out, in_=in_), reads=R, writes=W)

    def memset(eng, ap, val, W):
        return S.op(eng, lambda e: e.memset(ap, val), reads=(), writes=W)

    def recip(out, in_, R, W):
        return S.op("dve", lambda e: e.reciprocal(out=out, in_=in_), reads=R, writes=W)

    def scan(out, d0, d1, init, op0, op1, R, W):
        return S.op("dve", lambda e: e.tensor_tensor_scan(out=out, data0=d0, data1=d1, initial=init, op0=op0, op1=op1), reads=R, writes=W)

    def treduce(out, in_, op, R, W):
        return S.op("dve", lambda e: e.tensor_reduce(out=out, in_=in_, axis=AX.X, op=op), reads=R, writes=W)

    def tsingle(out, in_, scalar, op, R, W):
        return S.op("dve", lambda e: e.tensor_single_scalar(out=out, in_=in_, scalar=scalar, op=op), reads=R, writes=W)

    def vmax(out, in_, R, W):
        return S.op("dve", lambda e: e.max(out=out, in_=in_), reads=R, writes=W)

    chan_ids = {}

    def chan(name):
        if name not in chan_ids:
            assert len(chan_ids) < NCHAN, "out of dma channels"
            chan_ids[name] = f"c{len(chan_ids)}"
        return chan_ids[name]

    def bc(ap, shape):
        return ap.to_broadcast(list(shape))

    cast_state = {"n": 0, "last": {}}

    def cast_w(src, dst, rows, key, c0=None, c1=None):
        if c0 is not None:
            src = src[:, c0:c1]
            dst = dst[:, c0:c1]
        for r0 in range(0, rows, 128):
            r1 = min(rows, r0 + 128)
            ci = cast_state["n"] % 4
            cast_state["n"] += 1
            ch = chan(f"cast{ci}")
            S.no_barrier.add(ch)
            prev = cast_state["last"].get(ci)
            xd = [prev] if prev is not None else list(cast_state.get("xdma", []))
            idx = S.op("pool", lambda e, o=dst[r0:r1, :], i_=src[r0:r1, :]: e.dma_start(out=o, in_=i_), reads=[], writes=[(key, r0 // 128)],
                       chan=ch, extra_deps=xd, exact=True)
            cast_state["last"][ci] = idx

    x_dma = []
    for i4 in range(4):
        x_dma.append(dma("sp", x_sb[:, i4 * 4:(i4 + 1) * 4, :],
                         x_d[i4 * 512:(i4 + 1) * 512, :].rearrange("(i p) d -> p i d", p=128), [], [("x", i4 * 4 + k) for k in range(4)], chan("xin")))
    x_dma.append(dma("sp", g_rep[:], norm1_d[0, :].partition_broadcast(128), [], ["g_rep"], chan("grep")))
    cast_state["xdma"] = x_dma
    cast_state["g_preloaded"] = True
    dma("sp", ident_f[:], ident_d, [], ["ident_f"], chan("const"))
    dma("sp", tri125[:], tri_d, [], ["tri125"], chan("const"))
    dma("sp", b31_sb[:], b31_d, [], ["b31"], chan("const"))
    dma("sp", sel4[:], sel4_d, [], ["sel4"], chan("const"))
    dma("sp", pmask[:], pm_d, [], ["pmask"], chan("const"))
    for l in range(L):
        cast_w(w_in_d[l], wb_in[l], D_, f"wbinC{l}", 1032, 1544)
        cast_w(w_in_d[l], wb_in[l], D_, f"wbinM{l}", 0, 1032)
        cast_w(w_in_d[l], wb_in[l], D_, f"wbinA{l}", 1544, 3080)
        cast_w(w_out_d[l], wb_out[l], D_, f"wbout{l}")
        cast_w(w_up_d[l], wb_up[l], D_, f"wbup{l}")
        cast_w(w_dn_d[l], wb_dn[l], DFF, f"wbdn{l}")
    WBIN = lambda l, grp: [(f"wbin{grp}{l}", i) for i in range(8)]
    WBOUT = lambda l: [(f"wbout{l}", i) for i in range(8)]
    WBUP = lambda l: [(f"wbup{l}", i) for i in range(8)]
    WBDN = lambda l: [(f"wbdn{l}", i) for i in range(22)]

    cp("dve", ident_b[:], ident_f[:], ["ident_f"], ["ident_b"])
    memset("dve", ones_f[:], 1.0, ["ones_f"])
    memset("dve", onecol[:], 1.0 / 256.0, ["onecol"])
    with nc.sbuf_tensor("G_tmp", [128, 8, GU], F32) as G_tmp:
        dma("sp", G_tmp[:], G_d, [], ["G_tmp"], chan("const"))
        tt("dve", G_sb[:], G_tmp[:], bc(b31_sb[:].unsqueeze(2), [128, 8, GU]), ALU.subtract, ["G_tmp", "b31"], ["G_sb"])
        S.barrier()

    def norm_to_xT(g_dram_row, tag, pbase=0, use_pool=True):
        if cast_state.pop("g_preloaded", False):
            pass
        else:
            dma("sp", g_rep[:], g_dram_row.partition_broadcast(128), [], ["g_rep"], chan("grep"))
        with nc.sbuf_tensor(f"sqj_{tag}", [128, 2, D_], BF16) as sqj, \
             nc.sbuf_tensor(f"xs_{tag}", [128, 2, D_], BF16) as xs, \
             nc.sbuf_tensor(f"xf_{tag}", [128, 2, D_], F32) as xf:
            def s1(i):
                b = i % 2
                act(sqj[:, b, :], x_sb[:, i, :], AF.Square, [("x", i)], [("sqj", b), ("ss", i)], accum_out=ss[:, i:i + 1])
                act(rstd[:, i:i + 1], ss[:, i:i + 1], AF.Sqrt, [("ss", i)], [("rstd", i)], scale=1.0 / D_, bias=EPS)
                recip(rstd[:, i:i + 1], rstd[:, i:i + 1], [("rstd", i)], [("rstd", i)])

            def s2(i):
                b = i % 2
                if use_pool:
                    ts("pool", xf[:, b, :], x_sb[:, i, :], rstd[:, i:i + 1], 1.0, ALU.mult, ALU.mult, [("x", i), ("rstd", i)], [("xf", b)])
                    tt("pool", xs[:, b, :], xf[:, b, :], g_rep[:], ALU.mult, [("xf", b), "g_rep"], [("xs", b)])
                else:
                    stt(xs[:, b, :], x_sb[:, i, :], rstd[:, i:i + 1], g_rep[:], ALU.mult, ALU.mult,
                        [("x", i), ("rstd", i), "g_rep"], [("xs", b)])
                pb = pbase + i % 2
                for k in range(8):
                    tr(PSB[pb][:, k * 128:(k + 1) * 128], xs[:, b, k * 128:(k + 1) * 128], ident_b[:],
                       [("xs", b), "ident_b"], [pk(pb)])
                cp("act", xT[:, :, i * 128:(i + 1) * 128], PSB[pb][:].rearrange("p (k t) -> p k t", k=8),
                   [pk(pb)], [("xT", i)])

            s1(0)
            s1(1)
            s1(2)
            for i in range(NT):
                if i + 3 < NT:
                    s1(i + 3)
                s2(i)
            S.barrier()

    XT_ALL = [("xT", i) for i in range(NT)]

    for l in range(L):
        norm_to_xT(norm1_d[l, :], f"n1_{l}", use_pool=(l > 0))
        with ExitStack() as les:
            LE = les.enter_context
            mixT = LE(nc.sbuf_tensor(f"mixT{l}", [128, 8, S_], BF16))

            S.mute = phases is not None and 'conv' not in phases
            with ExitStack() as ces:
                CE = ces.enter_context
                wC = CE(nc.sbuf_tensor(f"wC{l}", [128, 8, 512], BF16))
                if l == 0:
                    with nc.sbuf_tensor("wC32", [128, 8, 512], F32) as wC32:
                        dma("sp", wC32[:], w_in_d[0][:, 1032:1544].rearrange("(k p) n -> p k n", p=128), [], ["wC32"], chan("wC"))
                        cp("dve", wC[:, 0:4, :], wC32[:, 0:4, :], ["wC32"], ["wC"])
                        cp("act", wC[:, 4:8, :], wC32[:, 4:8, :], ["wC32"], ["wC"])
                        S.barrier()
                ucv = CE(nc.sbuf_tensor(f"ucv{l}", [128, 2, 30 + S_], BF16))
                dgc = CE(nc.sbuf_tensor(f"dgc{l}", [128, 2, 31, 128], BF16))
                cwp = CE(nc.sbuf_tensor(f"cwp{l}", [128, 2, 31], F32))
                cvec = CE(nc.sbuf_tensor(f"cvec{l}", [128, 3, 2], F32))
                sg = CE(nc.sbuf_tensor(f"sg{l}", [128, 2, 512], F32))
                cv = CE(nc.sbuf_tensor(f"cv{l}", [128, 2, 512], F32))
                cvsq = CE(nc.sbuf_tensor(f"cvsq{l}", [128, 2, 512], F32))
                mean = CE(nc.sbuf_tensor(f"mean{l}", [128, 512], F32))
                var = CE(nc.sbuf_tensor(f"var{l}", [128, 512], F32))
                t1 = CE(nc.sbuf_tensor(f"t1{l}", [128, 2, 512], F32))
                if l > 0:
                    dma("sp", wC[:], wb_in[l][:, 1032:1544].rearrange("(k p) n -> p k n", p=128), WBIN(l, "C"), ["wC"], chan("wC"))
                dma("sp", cwp[:], cw_d[l], [], ["cwp"], chan("small"))
                dma("sp", cvec[:], cvec_d[l], [], ["cvec"], chan("small"))
                memset("dve", ucv[:, :, 0:30], 0.0, [("ucv", -1)])
                for fc in range(2):
                    for j in range(31):
                        ts("dve", dgc[:, fc, j, :], ident_f[:], cwp[:, fc, j:j + 1], None, ALU.mult, None,
                           ["ident_f", "cwp"], [("dgc", fc)])
                for c in range(4):
                    for fc in range(2):
                        pa, pg = (0, 1) if (c * 2 + fc) % 2 == 0 else (2, 3)
                        for k in range(8):
                            mm(PS[pa][:], wC[:, k, fc * 128:(fc + 1) * 128], xT[:, k, c * 512:(c + 1) * 512], k == 0, k == 7,
                               ["wC"] + XT_ALL[c * 4:c * 4 + 4], [pk(pa)])
                        for k in range(8):
                            mm(PS[pg][:], wC[:, k, 256 + fc * 128:256 + (fc + 1) * 128], xT[:, k, c * 512:(c + 1) * 512], k == 0, k == 7,
                               ["wC"] + XT_ALL[c * 4:c * 4 + 4], [pk(pg)])
                        sgb = (c * 2 + fc) % 2
                        act(sg[:, sgb, :], PS[pg][:], AF.Sigmoid, [pk(pg)], [("sg", sgb)])
                        tt("dve", ucv[:, fc, 30 + c * 512:30 + (c + 1) * 512], PS[pa][:], sg[:, sgb, :], ALU.mult,
                           [pk(pa), ("sg", sgb)], [("ucv", c)])
                for c in range(4):
                    rd = [("ucv", cc) for cc in range(-1, c + 1)]
                    for fc in range(2):
                        pc = 4 + fc
                        for j in range(31):
                            mm(PS[pc][:], dgc[:, fc, j, :], ucv[:, fc, c * 512 + j:c * 512 + j + 512], j == 0, j == 30,
                               rd + [("dgc", fc)], [pk(pc)])
                        act(cv[:, fc, :], PS[pc][:], AF.Identity, [pk(pc), "cvec"], [("cv", fc)], bias=cvec[:, 0, fc:fc + 1])
                        act(cvsq[:, fc, :], cv[:, fc, :], AF.Square, [("cv", fc)], [("cvsq", fc)])
                    for fc in range(2):
                        mm(PS[6][:], ones_f[:], cv[:, fc, :], fc == 0, fc == 1, ["ones_f", ("cv", fc)], [pk(6)])
                    for fc in range(2):
                        mm(PS[7][:], ones_f[:], cvsq[:, fc, :], fc == 0, fc == 1, ["ones_f", ("cvsq", fc)], [pk(7)])
                    ts("dve", mean[:], PS[6][:], 1.0 / 256, None, ALU.mult, None, [pk(6)], ["mean"])
                    ts("dve", var[:], PS[7][:], 1.0 / 256, EPS, ALU.mult, ALU.add, [pk(7)], ["var"])
                    tt("dve", t1[:, 0, :], mean[:], mean[:], ALU.mult, ["mean"], [("t1", 0)])
                    tt("dve", var[:], var[:], t1[:, 0, :], ALU.subtract, ["var", ("t1", 0)], ["var"])
                    act(var[:], var[:], AF.Sqrt, ["var"], ["var"])
                    recip(var[:], var[:], ["var"], ["var"])
                    for fc in range(2):
                        tt("dve", t1[:, fc, :], cv[:, fc, :], mean[:], ALU.subtract, [("cv", fc), "mean"], [("t1", fc)])
                        tt("dve", t1[:, fc, :], t1[:, fc, :], var[:], ALU.mult, [("t1", fc), "var"], [("t1", fc)])
                        act(mixT[:, 2 + fc, c * 512:(c + 1) * 512], t1[:, fc, :], AF.Silu, [("t1", fc), "cvec"], [("mixT", 2 + fc)],
                            bias=cvec[:, 2, fc:fc + 1], scale=cvec[:, 1, fc:fc + 1])
                S.barrier()

            S.mute = phases is not None and 'mlstm' not in phases
            with ExitStack() as mes:
                ME = mes.enter_context
                tok_sc = ME(nc.sbuf_tensor(f"tok_sc{l}", [128, NT, 8], F32))
                gam128 = ME(nc.sbuf_tensor(f"gam128{l}", [128, 16, 2], F32))
                hg_rep_t = ME(nc.sbuf_tensor(f"hg_rep{l}", [128, 512], F32))
                hg_rep = hg_rep_t[:, 0:256]
                wIF = ME(nc.sbuf_tensor(f"wIF{l}", [128, 8, 8], BF16))
                dma("sp", hg_rep, hg_d[l, :].partition_broadcast(128), [], ["hg_rep"], chan("small"))
                dma("sp", wIF[:], wb_in[l][:, 1024:1032].rearrange("(k p) n -> p k n", p=128), WBIN(l, "M"), ["wIF"], chan("wIF"))

                with ExitStack() as ges:
                    GE = ges.enter_context
                    gi = GE(nc.sbuf_tensor(f"gi{l}", [4, S_], F32))
                    gf = GE(nc.sbuf_tensor(f"gf{l}", [4, S_], F32))
                    gB = GE(nc.sbuf_tensor(f"gB{l}", [4, S_], F32))
                    gM = GE(nc.sbuf_tensor(f"gM{l}", [4, S_], F32))
                    gb2 = GE(nc.sbuf_tensor(f"gb2{l}", [4, 2], F32))
                    one4 = GE(nc.sbuf_tensor(f"one4{l}", [4, 1], F32))
                    gam = GE(nc.sbuf_tensor(f"gam{l}", [4, 16], F32))
                    gR = GE(nc.sbuf_tensor(f"gR{l}", [4, 16, 2], F32))
                    dma("sp", gb2[:, 0:1], igb_d[l], [], ["gb2"], chan("small"))
                    dma("sp", gb2[:, 1:2], fgb_d[l], [], ["gb2"], chan("small"))
                    memset("dve", one4[:], 1.0, ["one4"])
                    for c in range(4):
                        for k in range(8):
                            mm(PS[0][0:4, :], wIF[:, k, 0:4], xT[:, k, c * 512:(c + 1) * 512], k == 0, k == 7,
                               ["wIF"] + XT_ALL[c * 4:c * 4 + 4], [pk(0)])
                        for k in range(8):
                            mm(PS[1][0:4, :], wIF[:, k, 4:8], xT[:, k, c * 512:(c + 1) * 512], k == 0, k == 7,
                               ["wIF"] + XT_ALL[c * 4:c * 4 + 4], [pk(1)])
                        act(gi[:, c * 512:(c + 1) * 512], PS[0][0:4, :], AF.Identity, [pk(0), "gb2"], ["gi"], bias=gb2[:, 0:1])
                        act(gf[:, c * 512:(c + 1) * 512], PS[1][0:4, :], AF.Identity, [pk(1), "gb2"], ["gf"], bias=gb2[:, 1:2])
                    act(gf[:], gf[:], AF.Exp, ["gf"], ["gf"], scale=-1.0)
                    act(gf[:], gf[:], AF.Ln, ["gf"], ["gf"], bias=1.0)
                    scan(gB[:], bc(one4[:], [4, S_]), gf[:], 0.0, ALU.mult, ALU.subtract, ["gf", "one4"], ["gB"])
                    tt("dve", gi[:], gi[:], gB[:], ALU.subtract, ["gi", "gB"], ["gi"])
                    scan(gM[:], bc(one4[:], [4, S_]), gi[:], NEG, ALU.mult, ALU.max, ["gi", "one4"], ["gM"])
                    Mc = gM[:].rearrange("p (c t) -> p c t", t=128)[:, :, 127:128]
                    gi3 = gi[:].rearrange("p (c t) -> p c t", t=128)
                    gB3 = gB[:].rearrange("p (c t) -> p c t", t=128)
                    tt("dve", gi3, gi3, bc(Mc, [4, 16, 128]), ALU.subtract, ["gi", "gM"], ["gi"])
                    act(gi[:], gi[:], AF.Exp, ["gi"], ["gi"])
                    tt("dve", gB3, gB3, bc(Mc, [4, 16, 128]), ALU.add, ["gB", "gM"], ["gB"])
                    act(gB[:], gB[:], AF.Exp, ["gB"], ["gB"], scale=-1.0)
                    Mc2 = gM[:].rearrange("p (c t) -> p c t", t=128)[:, :, 127]
                    memset("dve", gam[:, 0:1], 0.0, ["gam"])
                    tt("dve", gam[:, 1:16], Mc2[:, 0:15], Mc2[:, 1:16], ALU.subtract, ["gM"], ["gam"])
                    act(gam[:, 1:16], gam[:, 1:16], AF.Exp, ["gam"], ["gam"])
                    memset("dve", gam[:, 0:1], 1.0, ["gam"])
                    tt("dve", gR[:], bc(gam[:].unsqueeze(2), [4, 16, 2]), pmask[:], ALU.mult, ["gam", "pmask"], ["gR"])
                    mm(PS[2][:, 0:32], sel4[:], gR[:].rearrange("p c q -> p (c q)"), True, True, ["sel4", "gR"], [pk(2)])
                    cp("dve", gam128[:].rearrange("p c q -> p (c q)"), PS[2][:, 0:32], [pk(2)], ["gam128"])
                    for i in range(NT):
                        tr(PS[3][:, i * 8:i * 8 + 4], gi[0:4, i * 128:(i + 1) * 128], ident_f[0:4, 0:4], ["gi", "ident_f"], [pk(3)])
                        tr(PS[3][:, i * 8 + 4:i * 8 + 8], gB[0:4, i * 128:(i + 1) * 128], ident_f[0:4, 0:4], ["gB", "ident_f"], [pk(3)])
                    cp("dve", tok_sc[:].rearrange("p i e -> p (i e)"), PS[3][:, 0:128], [pk(3)], ["tok_sc"])
                    S.barrier()

                if os.environ.get("MSTOP") == "M1":
                    S.mute = True
                qkT = ME(nc.sbuf_tensor(f"qkT{l}", [128, 4, S_], BF16))
                k_tok = ME(nc.sbuf_tensor(f"k_tok{l}", [128, NT, 256], BF16))
                wM = ME(nc.sbuf_tensor(f"wM{l}", [128, 8, 512], BF16))
                dma("sp", wM[:], wb_in[l][:, 0:512].rearrange("(k p) n -> p k n", p=128), WBIN(l, "M"), ["wM"], chan("wM"))
                with ExitStack() as qes:
                    QE = qes.enter_context
                    qkpre = QE(nc.sbuf_tensor(f"qkpre{l}", [128, 4, 3 + S_], BF16))
                    dgq = QE(nc.sbuf_tensor(f"dgq{l}", [128, 4, 4, 128], BF16))
                    qkw = QE(nc.sbuf_tensor(f"qkw{l}", [128, 4, 4], F32))
                    qkb = QE(nc.sbuf_tensor(f"qkb{l}", [128, 4], F32))
                    dma("sp", qkw[:], qkw_d[l], [], ["qkw"], chan("small"))
                    dma("sp", qkb[:], qkb_d[l], [], ["qkb"], chan("small"))
                    memset("dve", qkpre[:, :, 0:3], 0.0, [("qkpre", -1)])
                    for fc in range(4):
                        for j in range(4):
                            ts("dve", dgq[:, fc, j, :], ident_f[:], qkw[:, fc, j:j + 1], None, ALU.mult, None,
                               ["ident_f", "qkw"], [("dgq", fc)])
                    n = 0
                    for fc in range(4):
                        for c in range(4):
                            pb_ = n % 2; n += 1
                            for k in range(8):
                                mm(PS[pb_][:], wM[:, k, fc * 128:(fc + 1) * 128], xT[:, k, c * 512:(c + 1) * 512], k == 0, k == 7,
                                   ["wM"] + XT_ALL[c * 4:c * 4 + 4], [pk(pb_)])
                            cp("act" if n % 2 else "dve", qkpre[:, fc, 3 + c * 512:3 + (c + 1) * 512], PS[pb_][:], [pk(pb_)], [("qkpre", fc, c)])
                    for fc in range(4):
                        for c in range(4):
                            pb_ = 2 + (n % 2); n += 1
                            rdq = [("qkpre", -1)] + [("qkpre", fc, cc) for cc in range(c + 1)]
                            for j in range(4):
                                mm(PS[pb_][:], dgq[:, fc, j, :], qkpre[:, fc, c * 512 + j:c * 512 + j + 512], j == 0, j == 3,
                                   rdq + [("dgq", fc)], [pk(pb_)])
                            act(qkT[:, fc, c * 512:(c + 1) * 512], PS[pb_][:], AF.Silu, [pk(pb_), "qkb"], [("qkT", fc)], bias=qkb[:, fc:fc + 1])
                    for i in range(NT):
                        pb_ = 4 + (i % 2)
                        for kc in range(2):
                            tr(PSB[pb_][:, kc * 128:(kc + 1) * 128], qkT[:, 2 + kc, i * 128:(i + 1) * 128], ident_b[:],
                               [("qkT", 2 + kc), "ident_b"], [pk(pb_)])
                        cp("dve" if i % 2 else "act", k_tok[:, i, :], PSB[pb_][:, 0:256], [pk(pb_)], [("k_tok", i)])
                    S.barrier()

                if os.environ.get("MSTOP") == "M2":
                    S.mute = True
                vaug = ME(nc.sbuf_tensor(f"vaug{l}", [128, NT, 4, 66], BF16))
                og = ME(nc.sbuf_tensor(f"og{l}", [128, NT, 256], BF16))
                memset("dve", vaug[:, :, :, 64:66], 0.0, [("vaug", i) for i in range(NT)])
                dma("sp", wM[:], wb_in[l][:, 512:1024].rearrange("(k p) n -> p k n", p=128), WBIN(l, "M"), ["wM"], chan("wM"))
                with nc.sbuf_tensor(f"osig{l}", [128, 2, 256], F32) as osig:
                    for i in range(NT):
                        pb_ = i % 2
                        for k in range(8):
                            mm(PS[pb_][:], xT[:, k, i * 128:(i + 1) * 128], wM[:, k, :], k == 0, k == 7, ["wM", ("xT", i)], [pk(pb_)])
                        al = tok_sc[:, i, 0:4]
                        sk = os.environ.get("M3SKIP", "")
                        if "a" not in sk:
                            for h in range(4):
                                ts("dve", vaug[:, i, h, 0:64], PS[pb_][:, h * 64:(h + 1) * 64], tok_sc[:, i, h:h + 1], None, ALU.mult, None,
                                   [pk(pb_), "tok_sc"], [("vaug", i)])
                        if "b" not in sk:
                            cp("dve", vaug[:, i, :, 64:65], al.unsqueeze(2), ["tok_sc"], [("vaug", i)])
                        if "c" not in sk:
                            act(osig[:, pb_, :], PS[pb_][:, 256:512], AF.Sigmoid, [pk(pb_)], [("osig", pb_)])
                        if "d" not in sk:
                            tt("dve", og[:, i, :], osig[:, pb_, :], (g_rep[:, 0:256] if os.environ.get("HGTEST") else hg_rep), ALU.mult, [("osig", pb_), "hg_rep", "g_rep"], [("og", i)])
                    S.barrier()

                if os.environ.get("MSTOP") == "M3":
                    S.mute = True
                ym_tok = ME(nc.sbuf_tensor(f"ym_tok{l}", [128, NT, 256], BF16))
                with ExitStack() as ces2:
                    CE2 = ces2.enter_context
                    St = CE2(nc.sbuf_tensor(f"St{l}", [128, 2, 65], F32))
                    Sg = CE2(nc.sbuf_tensor(f"Sg{l}", [128, 2, 2, 66], BF16))
                    A_bf = CE2(nc.sbuf_tensor(f"A_bf{l}", [128, 2, 4, 128], BF16))
                    hm_t = CE2(nc.sbuf_tensor(f"hm{l}", [128, 3, 4, 64], F32))
                    hsq_t = CE2(nc.sbuf_tensor(f"hsq{l}", [128, 1, 4, 64], F32))
                    st4_t = CE2(nc.sbuf_tensor(f"st4{l}", [128, 3, 8, 4], F32))
                    memset("dve", St[:], 0.0, ["St"])

                    def rec_stage(c):
                        cb = c % 2
                        csl = slice(c * 128, (c + 1) * 128)
                        for pr in range(2):
                            mm(PS[4 + pr][:, 0:264], k_tok[:, c, pr * 128:(pr + 1) * 128],
                               vaug[:, c, :, :].rearrange("p h e -> p (h e)"), True, True, [("k_tok", c), ("vaug", c)], [pk(4 + pr)])
                        for pr in range(2):
                            ts("dve", Sg[:, cb, pr, 0:65], St[:, pr, :], gam128[:, c, pr:pr + 1], 0.125, ALU.mult, ALU.mult,
                               ["St", "gam128"], [("Sg", cb)])
                        for h in range(4):
                            pb0 = (h % 2) * 64
                            pr = h // 2
                            stt(St[pb0:pb0 + 64, pr, :], St[pb0:pb0 + 64, pr, :], gam128[pb0:pb0 + 64, c, pr:pr + 1],
                                PS[4 + pr][pb0:pb0 + 64, h * 66:h * 66 + 65], ALU.mult, ALU.add, ["St", "gam128", pk(4 + pr)], ["St"])
                        for par in range(2):
                            pb0 = par * 64
                            pa = 0 + par
                            for q in range(2):
                                mm(PS[pa][:, q * 128:(q + 1) * 128], qkT[pb0:pb0 + 64, 2 + q, csl], qkT[pb0:pb0 + 64, q, csl], True, True,
                                   [("qkT", 2 + q), ("qkT", q)], [pk(pa)])
                            for q in range(2):
                                tt("dve", A_bf[:, cb, 2 * par + q, :], PS[pa][:, q * 128:(q + 1) * 128], tri125[:], ALU.mult,
                                   [pk(pa), "tri125"], [("A_bf", cb, par)])
                        for par in range(2):
                            pb0 = par * 64
                            pacc = (2 if cb == 0 else 6) + par
                            for q in range(2):
                                h = 2 * q + par
                                mm(PS[pacc][:, q * 65:(q + 1) * 65], A_bf[:, cb, 2 * par + q, :], vaug[:, c, h, 0:65], True, False,
                                   [("A_bf", cb, par), ("vaug", c)], [pk(pacc)])
                                mm(PS[pacc][:, q * 65:(q + 1) * 65], qkT[pb0:pb0 + 64, q, csl], Sg[pb0:pb0 + 64, cb, q, 0:65], False, True,
                                   [("qkT", q), ("Sg", cb)], [pk(pacc)])

                    def post_a(c):
                        cb = c % 2
                        c3 = c % 3
                        hm = hm_t[:, c3, :, 0:64]
                        hsq = hsq_t[:, 0]
                        st4 = st4_t[:, c3]
                        for par in range(2):
                            pacc = (2 if cb == 0 else 6) + par
                            acc3 = PS[pacc][:, 0:130].rearrange("p (h e) -> p h e", h=2)
                            sl = slice(2 * par, 2 * par + 2)
                            act(st4[:, 0, sl], acc3[:, :, 64], AF.Abs, [pk(pacc)], [("st4", c3, 0)])
                            tt("dve", st4[:, 0, sl], st4[:, 0, sl], tok_sc[:, c, 4 + par:8:2], ALU.max, [("st4", c3, 0), "tok_sc"], [("st4", c3, 0)])
                            recip(st4[:, 1, sl], st4[:, 0, sl], [("st4", c3, 0)], [("st4", c3, 1)])
                            for q in range(2):
                                ts("dve", hm[:, 2 * par + q, :], acc3[:, q, 0:64], st4[:, 1, 2 * par + q:2 * par + q + 1], None, ALU.mult, None,
                                   [pk(pacc), ("st4", c3, 1)], [("hm", c3)])
                        treduce(st4[:, 2, :], hm[:, :, :], ALU.add, [("hm", c3)], [("st4", c3, 2)])
                        tt("pool", hsq[:, :, 0:64], hm[:, :, :], hm[:, :, :], ALU.mult, [("hm", c3)], [("hsq", 0)])

                    def post_b(c):
                        c3 = c % 3
                        hsq = hsq_t[:, 0]
                        st4 = st4_t[:, c3]
                        treduce(st4[:, 3, :], hsq[:, :, 0:64], ALU.add, [("hsq", 0)], [("st4", c3, 3)])
                        ts("dve", st4[:, 2, :], st4[:, 2, :], 1.0 / 64, None, ALU.mult, None, [("st4", c3, 2)], [("st4", c3, 2)])
                        tt("dve", st4[:, 4, :], st4[:, 2, :], st4[:, 2, :], ALU.mult, [("st4", c3, 2)], [("st4", c3, 4)])
                        stt(st4[:, 5, :], st4[:, 3, :], 1.0 / 64, st4[:, 4, :], ALU.mult, ALU.subtract, [("st4", c3, 3), ("st4", c3, 4)], [("st4", c3, 5)])
                        act(st4[:, 5, :], st4[:, 5, :], AF.Sqrt, [("st4", c3, 5)], [("st4", c3, 5)], bias=EPS)

                    def post_c(c):
                        c3 = c % 3
                        hm = hm_t[:, c3, :, 0:64]
                        st4 = st4_t[:, c3]
                        recip(st4[:, 6, :], st4[:, 5, :], [("st4", c3, 5)], [("st4", c3, 6)])
                        tt("pool", hm[:, :, :], hm[:, :, :], bc(st4[:, 2, :].unsqueeze(2), [128, 4, 64]), ALU.subtract, [("hm", c3), ("st4", c3, 2)], [("hm", c3)])
                        tt("pool", hm[:, :, :], hm[:, :, :], bc(st4[:, 6, :].unsqueeze(2), [128, 4, 64]), ALU.mult, [("hm", c3), ("st4", c3, 6)], [("hm", c3)])
                        for par in range(2):
                            ogv = og[:, c, :].rearrange("p (h d) -> p h d", h=4)[:, par:4:2, :]
                            ymv = ym_tok[:, c, :].rearrange("p (h d) -> p h d", h=4)[:, par:4:2, :]
                            tt("pool", ymv, hm[:, 2 * par:2 * par + 2, :], ogv, ALU.mult, [("hm", c3), ("og", c)], [("ym_tok", c)])

                    for c in range(NT + 2):
                        if c < NT:
                            rec_stage(c)
                        if 1 <= c <= NT:
                            post_a(c - 1)
                        if c >= 2:
                            post_c(c - 2)
                        if 1 <= c <= NT:
                            post_b(c - 1)
                    for i in range(NT):
                        pb_ = i % 2
                        for kc in range(2):
                            tr(PSB[pb_][:, kc * 128:(kc + 1) * 128], ym_tok[:, i, kc * 128:(kc + 1) * 128], ident_b[:],
                               [("ym_tok", i), "ident_b"], [pk(pb_)])
                        cp("act" if i % 2 else "dve", mixT[:, 0:2, i * 128:(i + 1) * 128],
                           PSB[pb_][:, 0:256].rearrange("p (k t) -> p k t", k=2), [pk(pb_)], [("mixT", 0), ("mixT", 1)])
                    S.barrier()

            S.mute = phases is not None and 'attn' not in phases
            for g in range(2):
                with ExitStack() as aes:
                    AE = aes.enter_context
                    QT = AE(nc.sbuf_tensor(f"QT{l}{g}", [72, 4, S_], BF16))
                    KT = AE(nc.sbuf_tensor(f"KT{l}{g}", [72, 4, S_], BF16))
                    Vaug = AE(nc.sbuf_tensor(f"Vaug{l}{g}", [128, NT, 4, 66], BF16))
                    kmh = AE(nc.sbuf_tensor(f"kmh{l}{g}", [64, 4, 16], BF16))
                    for hh in range(4):
                        dma("sp", KT[64:72, hh, :], oneh_d, [], [("KT", hh)], chan("small"))
                    memset("dve", Vaug[:, :, :, 64:65], 1.0, [("Vaug", i) for i in range(NT)])
                    with ExitStack() as a1:
                        A1 = a1.enter_context
                        wA = A1(nc.sbuf_tensor(f"wA{l}{g}", [128, 8, 768], BF16))
                        qk_t = A1(nc.sbuf_tensor(f"qk_t{l}{g}", [128, 2, 512], BF16))
                        km_f = A1(nc.sbuf_tensor(f"km_f{l}{g}", [128, 2, 8], F32))
                        kmT2 = A1(nc.sbuf_tensor(f"kmT2{l}{g}", [128, 2, 16], BF16))
                        km_t = A1(nc.sbuf_tensor(f"km_t{l}{g}", [128, 2, 8], F32))
                        for part, c0 in enumerate((1544, 2056, 2568)):
                            dma("sp", wA[:, :, part * 256:(part + 1) * 256],
                                wb_in[l][:, c0 + g * 256:c0 + (g + 1) * 256].rearrange("(k p) n -> p k n", p=128), WBIN(l, "A"), ["wA"], chan("wA"))
                        def a1_proj(i):
                            b = i % 2
                            pq, pv = 0 + b, 2 + b
                            for k in range(8):
                                mm(PS[pq][:], xT[:, k, i * 128:(i + 1) * 128], wA[:, k, 0:512], k == 0, k == 7, ["wA", ("xT", i)], [pk(pq)])
                            for k in range(8):
                                mm(PS[pv][:, 0:256], xT[:, k, i * 128:(i + 1) * 128], wA[:, k, 512:768], k == 0, k == 7, ["wA", ("xT", i)], [pk(pv)])
                            act(qk_t[:, b, 0:256], PS[pq][:, 0:256], AF.Copy, [pk(pq)], [("qk_t", b)], scale=0.125)
                            cp("dve", qk_t[:, b, 256:512], PS[pq][:, 256:512], [pk(pq)], [("qk_t", b)])
                            cp("act", Vaug[:, i, :, 0:64], PS[pv][:, 0:256].rearrange("p (h d) -> p h d", h=4), [pk(pv)], [("Vaug", i)])

                        def a1_tr(i):
                            b = i % 2
                            ptq = 4 + b
                            for hh in range(4):
                                tr(PSB[ptq][0:64, hh * 128:(hh + 1) * 128], qk_t[:, b, hh * 64:(hh + 1) * 64], ident_b[:],
                                   [("qk_t", b), "ident_b"], [pk(ptq)])
                                tr(PSB[ptq][0:64, 512 + hh * 128:512 + (hh + 1) * 128], qk_t[:, b, 256 + hh * 64:256 + (hh + 1) * 64], ident_b[:],
                                   [("qk_t", b), "ident_b"], [pk(ptq)])
                            cp("act", QT[0:64, :, i * 128:(i + 1) * 128], PSB[ptq][0:64, 0:512].rearrange("p (h t) -> p h t", h=4),
                               [pk(ptq)], [("QT", hh) for hh in range(4)])
                            cp("dve", KT[0:64, :, i * 128:(i + 1) * 128], PSB[ptq][0:64, 512:1024].rearrange("p (h t) -> p h t", h=4),
                               [pk(ptq)], [("KT", hh) for hh in range(4)])
                            nb = i // 2
                            for pr in range(2):
                                mm(PS[6 + pr][:, nb * 2:nb * 2 + 2], qk_t[:, b, 256 + pr * 128:256 + (pr + 1) * 128], onecol[:],
                                   i % 2 == 0, i % 2 == 1, [("qk_t", b), "onecol"], [pk(6 + pr)])

                        a1_proj(0)
                        for i in range(NT):
                            if i + 1 < NT:
                                a1_proj(i + 1)
                            a1_tr(i)
                        for pr in range(2):
                            cp("dve", km_f[:, pr, :], PS[6 + pr][:, 0:16].rearrange("p (n two) -> p n two", two=2)[:, :, 0], [pk(6 + pr)], ["km_f"])
                        cp("dve", kmT2[:, :, 0:8], km_f[:], ["km_f"], ["kmT2"])
                        cp("dve", km_t[:], kmT2[:, :, 0:8], ["kmT2"], ["km_t"])
                        tt("dve", kmT2[:, :, 8:16], km_f[:], km_t[:], ALU.subtract, ["km_f", "km_t"], ["kmT2"])
                        for hh in range(4):
                            pb0 = (hh % 2) * 64
                            cp("dve", kmh[:, hh, :], kmT2[pb0:pb0 + 64, hh // 2, :], ["kmT2"], ["kmh"])
                        S.barrier()

                    with ExitStack() as a2:
                        A2 = a2.enter_context
                        gsb = A2(nc.sbuf_tensor(f"gsb{l}{g}", [128, NT, 4, 16], F32))
                        gate = A2(nc.sbuf_tensor(f"gate{l}{g}", [128, NT * 4, 8], F32))
                        cmpt = A2(nc.sbuf_tensor(f"cmpt{l}{g}", [128, NT * 4, 8, 8], BF16))
                        rank = A2(nc.sbuf_tensor(f"rank{l}{g}", [128, NT * 4, 8], F32))
                        mval = A2(nc.sbuf_tensor(f"mval{l}{g}", [128, NT, 4, 8], BF16))
                        for i in range(NT):
                            b = i // 8
                            for hh in range(4):
                                co = (i % 8) * 64 + hh * 16
                                mm(PS[b][:, co:co + 16], QT[0:64, hh, i * 128:(i + 1) * 128], kmh[:, hh, :], True, True,
                                   [("QT", hh), "kmh"], [pk(b)])
                        for b in range(2):
                            cp("act", gsb[:, b * 8:(b + 1) * 8].rearrange("p i h e -> p (i h e)"), PS[b][:, 0:512], [pk(b)], ["gsb"])
                        gsbv = gsb[:].rearrange("p i h e -> p (i h) e")
                        tt("dve", gate[:], gsbv[:, :, 0:8], gsbv[:, :, 8:16], ALU.add, ["gsb"], ["gate"])
                        for nb in range(8):
                            memset("dve", gate[:, nb * 8:(nb + 1) * 8, nb:8], NEG, ["gate"])
                        tt("dve", cmpt[:], bc(gate[:].unsqueeze(3), [128, NT * 4, 8, 8]), bc(gate[:].unsqueeze(2), [128, NT * 4, 8, 8]), ALU.is_lt,
                           ["gate"], ["cmpt"])
                        treduce(rank[:], cmpt[:], ALU.add, ["cmpt"], ["rank"])
                        ts("dve", mval[:].rearrange("p i h e -> p (i h) e"), rank[:], 3.0, NEG, ALU.is_ge, ALU.mult, ["rank"], [("mval", i) for i in range(NT)])
                        for nb in range(8):
                            memset("dve", mval[:, 2 * nb:2 * nb + 2, :, nb:nb + 1], 0.0, [("mval", 2 * nb), ("mval", 2 * nb + 1)])
                        for hh in range(4):
                            for half in range(2):
                                pm_ = 2 + (hh * 2 + half) % 2
                                for ii in range(8):
                                    i = half * 8 + ii
                                    tr(PSB[pm_][0:8, ii * 128:(ii + 1) * 128], mval[:, i, hh, :], ident_b[:], [("mval", i), "ident_b"], [pk(pm_)])
                                cp("act" if half else "dve", QT[64:72, hh, half * 1024:(half + 1) * 1024], PSB[pm_][0:8, :], [pk(pm_)], [("QT", hh)])
                        S.barrier()

                    with ExitStack() as a3:
                        A3 = a3.enter_context
                        PT = A3(nc.sbuf_tensor(f"PT{l}{g}", [128, 4, 512], BF16))
                        tmpn = A3(nc.sbuf_tensor(f"tmpn{l}{g}", [128, 3, 256], F32))
                        ya_tok = A3(nc.sbuf_tensor(f"ya_tok{l}{g}", [128, NT, 256], BF16))
                        rden = A3(nc.sbuf_tensor(f"rden{l}{g}", [128, 4], F32))
                        iters = [(hh, c, j) for hh in range(4) for c in range(4) for j in range(4 * c + 4)]
                        SBANK = [0, 1, 6, 7]

                        def st_stage(n):
                            hh, c, j = iters[n]
                            b = n % 4
                            c0 = max(0, j - 4 * c) * 128
                            mm(PS[SBANK[b]][:, c0:512], KT[0:72, hh, j * 128:(j + 1) * 128], QT[0:72, hh, c * 512 + c0:(c + 1) * 512], True, True,
                               [("KT", hh), ("QT", hh)], [pk(SBANK[b])])

                        def pv_stage(n):
                            hh, c, j = iters[n]
                            hg_ = g * 4 + hh
                            b = n % 4
                            psb_ = SBANK[b]
                            tt0 = max(0, j - 4 * c)
                            c0 = tt0 * 128
                            n0 = max(c0, 128 * j - 512 * c)
                            n1 = min(512, 128 * j - 512 * c + 256)
                            if n1 > n0:
                                u0 = 512 * c + n0 - 128 * j
                                tt("dve", PS[psb_][:, n0:n1], PS[psb_][:, n0:n1], G_sb[:, hg_, u0:u0 + (n1 - n0)], ALU.add,
                                   [pk(psb_), "G_sb"], [pk(psb_)])
                            act(PT[:, b, c0:512], PS[psb_][:, c0:512], AF.Exp, [pk(psb_), "b31"], [("PT", b)], bias=b31_sb[:, hg_:hg_ + 1])
                            for tti in range(tt0, 4):
                                pacc = 2 + tti
                                mm(PS[pacc][:, 0:65], PT[:, b, tti * 128:(tti + 1) * 128], Vaug[:, j, hh, 0:65], j == 0, j == 4 * c + tti,
                                   [("PT", b), ("Vaug", j)], [pk(pacc)])
                            if j == 4 * c + 3:
                                for tti in range(4):
                                    pacc = 2 + tti
                                    i = 4 * c + tti
                                    recip(rden[:, tti:tti + 1], PS[pacc][:, 64:65], [pk(pacc)], [("rden", tti)])
                                    ts("dve", ya_tok[:, i, hh * 64:(hh + 1) * 64], PS[pacc][:, 0:64], rden[:, tti:tti + 1], None, ALU.mult, None,
                                       [pk(pacc), ("rden", tti)], [("ya_tok", i)])

                        NIT = len(iters)
                        st_stage(0)
                        st_stage(1)
                        st_stage(2)
                        for n in range(NIT):
                            if n + 3 < NIT:
                                st_stage(n + 3)
                            pv_stage(n)
                        for i in range(NT):
                            pb_ = 7 if i % 2 else 0
                            for kc in range(2):
                                tr(PSB[pb_][:, kc * 128:(kc + 1) * 128], ya_tok[:, i, kc * 128:(kc + 1) * 128], ident_b[:],
                                   [("ya_tok", i), "ident_b"], [pk(pb_)])
                            cp("act" if i % 2 else "dve", mixT[:, 4 + 2 * g:6 + 2 * g, i * 128:(i + 1) * 128],
                               PSB[pb_][:, 0:256].rearrange("p (k t) -> p k t", k=2), [pk(pb_)], [("mixT", 4 + 2 * g), ("mixT", 5 + 2 * g)])
                        S.barrier()

            S.mute = False
            if debug == "mix" and l == 0:
                dma("sp", dbg_d, mixT[:], [("mixT", k) for k in range(8)], ["dbg"], chan("out"))
                S.barrier()

            S.mute = phases is not None and 'out' not in phases
            with nc.sbuf_tensor(f"wO{l}", [128, 8, D_], BF16) as wO:
                dma("sp", wO[:], wb_out[l].rearrange("(k p) n -> p k n", p=128), WBOUT(l), ["wO"], chan("wO"))
                for i in range(NT):
                    for half in range(2):
                        pb_ = (i * 2 + half) % 4
                        for k in range(8):
                            mm(PS[pb_][:], mixT[:, k, i * 128:(i + 1) * 128], wO[:, k, half * 512:(half + 1) * 512], k == 0, k == 7,
                               ["wO", ("mixT", k)], [pk(pb_)])
                        tt("dve", x_sb[:, i, half * 512:(half + 1) * 512], x_sb[:, i, half * 512:(half + 1) * 512], PS[pb_][:], ALU.add,
                           [pk(pb_), ("x", i)], [("x", i)])
                if phases is None or 'ffn' in phases:
                    norm_to_xT(norm2_d[l, :], f"n2_{l}", pbase=4)
                S.barrier()

        S.mute = False
        if debug == "x1" and l == 0:
            dma("sp", dbg_d.rearrange("(i p) d -> p i d", p=128), x_sb[:], [("x", i) for i in range(NT)], ["dbg"], chan("out"))

        S.mute = phases is not None and 'ffn' not in phases
        with ExitStack() as fes:
            FE = fes.enter_context
            actT = FE(nc.sbuf_tensor(f"actT{l}", [128, NJ, 512], BF16))
            wU = FE(nc.sbuf_tensor(f"wU{l}", [128, 2, 2, 8, 256], BF16))
            wD = FE(nc.sbuf_tensor(f"wD{l}", [128, 2, NJ, 512], BF16))
            upre = FE(nc.sbuf_tensor(f"upre{l}", [128, 2, 2, 514], BF16))
            halo = FE(nc.sbuf_tensor(f"halo{l}", [128, 2 * NJ, 2], BF16))
            dgf = FE(nc.sbuf_tensor(f"dgf{l}", [128, 2, 2, 3, 128], BF16))
            fcw = FE(nc.sbuf_tensor(f"fcw{l}", [128, 2 * NJ, 3], F32))
            fcb = FE(nc.sbuf_tensor(f"fcb{l}", [128, 2 * NJ], F32))
            sgf = FE(nc.sbuf_tensor(f"sgf{l}", [128, 2, 512], F32))
            dma("sp", fcw[:], fcw_d[l], [], ["fcw"], chan("small"))
            dma("sp", fcb[:], fcb_d[l], [], ["fcb"], chan("small"))
            memset("dve", halo[:], 0.0, [("halo", jx) for jx in range(NJ)])
            halo4 = halo[:].rearrange("p (h j) e -> p h j e", h=2)
            fcw4 = fcw[:].rearrange("p (h j) e -> p h j e", h=2)
            for half in range(2):
                dma("sp", wD[:, half, 0:21, :], wb_dn[l][0:2688, half * 512:(half + 1) * 512].rearrange("(j p) n -> p j n", p=128),
                    WBDN(l), [("wD", half)], chan("wD"))
                dma("sp", wD[0:64, half, 21, :], wb_dn[l][2688:2752, half * 512:(half + 1) * 512], WBDN(l), [("wD", half)], chan("wD"))
            for c in range(4):
                def up_stage(j):
                    j2 = j // 2
                    jj = j % 2
                    wb_ = j2 % 2
                    ub = j % 2
                    wj = 128 if j < 21 else 64
                    if jj == 0:
                        ncols = 256 if j2 < 10 else 192
                        for half in range(2):
                            col0 = half * DFF + j2 * 256
                            dma("sp", wU[:, wb_, half, :, 0:ncols], wb_up[l][:, col0:col0 + ncols].rearrange("(k p) n -> p k n", p=128),
                                WBUP(l), [("wU", wb_)], chan(f"wU{wb_}"))
                    tt("dve", dgf[0:wj, ub, :, :, 0:wj],
                       bc(ident_f[0:wj, 0:wj].unsqueeze(1).unsqueeze(1), [wj, 2, 3, wj]),
                       bc(fcw4[0:wj, :, j, :].unsqueeze(3), [wj, 2, 3, wj]), ALU.mult,
                       ["ident_f", "fcw"], [("dgf", ub, 0), ("dgf", ub, 1)])
                    for half in range(2):
                        pu = half + 2 * ub
                        for k in range(8):
                            mm(PS[pu][0:wj, :], wU[:, wb_, half, k, jj * 128:jj * 128 + wj], xT[:, k, c * 512:(c + 1) * 512], k == 0, k == 7,
                               [("wU", wb_)] + XT_ALL[c * 4:c * 4 + 4], [pk(pu)])
                        cp("act", upre[0:wj, ub, half, 2:514], PS[pu][0:wj, :], [pk(pu)], [("upre", ub, half)])
                    cp("dve", upre[0:wj, ub, :, 0:2], halo4[0:wj, :, j, :], [("halo", j)], [("upre", ub, 0), ("upre", ub, 1)])
                    cp("dve", halo4[0:wj, :, j, :], upre[0:wj, ub, :, 512:514], [("upre", ub, 0), ("upre", ub, 1)], [("halo", j)])

                def conv_stage(j):
                    ub = j % 2
                    wj = 128 if j < 21 else 64
                    for half in range(2):
                        pcv = 4 + half + 2 * ub
                        for tap in range(3):
                            mm(PS[pcv][0:wj, :], dgf[0:wj, ub, half, tap, 0:wj], upre[0:wj, ub, half, tap:tap + 512], tap == 0, tap == 2,
                               [("dgf", ub, half), ("upre", ub, half)], [pk(pcv)])
                    act(sgf[0:wj, ub, :], PS[4 + 2 * ub][0:wj, :], AF.Silu, [pk(4 + 2 * ub), "fcb"], [("sgf", ub)], bias=fcb[0:wj, j:j + 1])
                    stt(actT[0:wj, j, :], PS[5 + 2 * ub][0:wj, :], fcb[0:wj, NJ + j:NJ + j + 1], sgf[0:wj, ub, :], ALU.add, ALU.mult,
                        [pk(5 + 2 * ub), "fcb", ("sgf", ub)], [("actT", j)])

                up_stage(0)
                for j in range(NJ):
                    if j + 1 < NJ:
                        up_stage(j + 1)
                    conv_stage(j)
                for tti in range(4):
                    i = 4 * c + tti
                    for half in range(2):
                        pd_ = half
                        for j in range(NJ):
                            wj = 128 if j < 21 else 64
                            mm(PS[pd_][:], actT[0:wj, j, tti * 128:(tti + 1) * 128], wD[0:wj, half, j, :], j == 0, j == NJ - 1,
                               [("actT", j), ("wD", half)], [pk(pd_)])
                        tt("dve", x_sb[:, i, half * 512:(half + 1) * 512], x_sb[:, i, half * 512:(half + 1) * 512], PS[pd_][:], ALU.add,
                           [pk(pd_), ("x", i)], [("x", i)])
            S.barrier()
        S.mute = False
        if debug == "x2" and l == 0:
            dma("sp", dbg_d.rearrange("(i p) d -> p i d", p=128), x_sb[:], [("x", i) for i in range(NT)], ["dbg"], chan("out"))

    dma("sp", g_rep[:], final_d[0, :].partition_broadcast(128), [], ["g_rep"], chan("grep"))
    with nc.sbuf_tensor("sqj_f", [128, D_], BF16) as sqj, nc.sbuf_tensor("yo", [128, 2, D_], F32) as yo:
        for i in range(NT):
            b = i % 2
            act(sqj[:], x_sb[:, i, :], AF.Square, [("x", i)], [("ss", i)], accum_out=ss[:, i:i + 1])
            ts("dve", rstd[:, i:i + 1], ss[:, i:i + 1], 1.0 / D_, EPS, ALU.mult, ALU.add, [("ss", i)], [("rstd", i)])
            act(rstd[:, i:i + 1], rstd[:, i:i + 1], AF.Sqrt, [("rstd", i)], [("rstd", i)])
            recip(rstd[:, i:i + 1], rstd[:, i:i + 1], [("rstd", i)], [("rstd", i)])
            stt(yo[:, b, :], x_sb[:, i, :], rstd[:, i:i + 1], g_rep[:], ALU.mult, ALU.mult, [("x", i), ("rstd", i), "g_rep"], [("yo", b)])
            dma("sp", out_d[i * 128:(i + 1) * 128, :], yo[:, b, :], [("yo", b)], [("out", i)], chan(f"out{b}"))
        fin_reads = [("out", i) for i in range(NT)] + (["dbg"] if dbg_d is not None else [])
        S.op("sp", None, reads=fin_reads, writes=[])
        S.barrier()
    chan_sems = {v: chan_sem_list[int(v[1:])] for v in chan_ids.values()}
    S.emit(sems, chan_sems, block)
    es.close()
    return nc


def make_in_maps(inputs):
    f = lambda a: np.ascontiguousarray(np.asarray(a, dtype=np.float32))
    c = host_constants()
    G, b31 = t5_tables(f(inputs["rel_bias"]))
    fcw = f(inputs["ffn_conv_w"])
    fcb = f(inputs["ffn_conv_b"])
    fcw_p = np.zeros((2, 2, NJ * 128, 3), np.float32)
    fcb_p = np.zeros((2, 2, NJ * 128), np.float32)
    for half in range(2):
        fcw_p[:, half, :DFF, :] = np.transpose(fcw[:, :, half * DFF:(half + 1) * DFF], (0, 2, 1))
        fcb_p[:, half, :DFF] = fcb[:, half * DFF:(half + 1) * DFF]
    pc = lambda a, nchunk: np.ascontiguousarray(np.moveaxis(a.reshape((2, nchunk, 128) + a.shape[2:]), 1, 2))
    qkw_h = pc(np.transpose(f(inputs["mlstm_qk_conv_w"]), (0, 2, 1)), 4)
    qkb_h = pc(f(inputs["mlstm_qk_conv_b"]), 4)
    cfw_h = pc(np.transpose(f(inputs["conf_dw_w"]), (0, 2, 1)), 2)
    cvec_h = np.ascontiguousarray(np.stack([pc(f(inputs["conf_dw_b"]), 2), pc(f(inputs["conf_ln_g"]), 2),
                                            pc(f(inputs["conf_ln_b"]), 2)], axis=2))
    fcw_h = pc(fcw_p.reshape(2, 2 * NJ * 128, 3), 2 * NJ)
    fcb_h = pc(fcb_p.reshape(2, 2 * NJ * 128), 2 * NJ)
    shared = {
        "w_in": f(inputs["w_in"]), "w_out": f(inputs["w_out"]),
        "ffn_w_up": f(inputs["ffn_w_up"]), "ffn_w_down": f(inputs["ffn_w_down"]),
        "norm1_g": f(inputs["norm1_g"]), "norm2_g": f(inputs["norm2_g"]),
        "final_g": f(inputs["final_g"]).reshape(1, D_),
        "qkw": qkw_h, "qkb": qkb_h,
        "igb": f(inputs["mlstm_ig_b"]).reshape(2, 4, 1),
        "fgb": f(inputs["mlstm_fg_b"]).reshape(2, 4, 1),
        "headg": f(inputs["mlstm_head_g"]),
        "cfw": cfw_h, "cvec": cvec_h, "fcw": fcw_h, "fcb": fcb_h,
        "gt5": G, "b31": b31,
        "ident": c["ident"], "tri125": c["tri125"], "blk_onehot": c["blk_onehot"],
        "sel4": c["sel4"], "pairmask": c["pairmask"],
    }
    x = f(inputs["x"])
    return [dict(shared, x=np.ascontiguousarray(x[b])) for b in range(x.shape[0])]


_NC_CACHE = {}


def kernel(**inputs):
    in_maps = make_in_maps(inputs)
    if "nc" not in _NC_CACHE:
        _NC_CACHE["nc"] = build(2)
    res = run_bass_kernel_spmd(_NC_CACHE["nc"], in_maps, core_ids=list(range(8)))
    return np.stack([r["out"] for r in res.results], axis=0).astype(np.float32)
```
